# Optimizing a Trainium2 kernel written in Bass

```python
import jax, jax.numpy as jnp
from jax import lax
import numpy as np

D_MODEL = 2048
BATCH = 2
SEQ = 8192
DEPTH = 1
DEC_BATCH = 128
DEC_SEQ = 1
PAST_LEN = 16384
PAGE_SIZE = 128

D_MIX = D_MODEL
D_POOL = D_MIX // 2
POOL_WINDOWS = (2, 4, 8, 16)
N_POOL_GROUPS = len(POOL_WINDOWS)
POOL_GROUP = D_POOL // N_POOL_GROUPS
POOL_STATE = max(POOL_WINDOWS) - 1
HEAD_DIM = 64
D_ATTN = D_MIX - D_POOL
N_HEADS = D_ATTN // HEAD_DIM
N_KV_HEADS = max(1, N_HEADS // 8)
GQA_GROUP = N_HEADS // N_KV_HEADS
KV_W = N_KV_HEADS * HEAD_DIM
WINDOW = 128
BLOCK = 128
ROPE_DIM = HEAD_DIM // 4
ROPE_THETA = 500000.0
EPS = 1e-5
N_IN = 2 * D_POOL + 2 * D_ATTN + 2 * KV_W

kernel_name = 'hybrid_pool_swa_sink_step'


def rms_norm(x, g):
    xf = x.astype(jnp.float32)
    y = xf * lax.rsqrt(jnp.mean(xf * xf, axis=-1, keepdims=True) + EPS)
    return (y * g.astype(jnp.float32)).astype(x.dtype)


def split_cols(z):
    o1 = D_POOL
    o2 = o1 + D_POOL
    o3 = o2 + D_ATTN
    o4 = o3 + KV_W
    o5 = o4 + KV_W
    return z[..., :o1], z[..., o1:o2], z[..., o2:o3], z[..., o3:o4], z[..., o4:o5], z[..., o5:]


def rope_partial(x, pos):
    half = ROPE_DIM // 2
    inv = jnp.power(jnp.float32(ROPE_THETA), -jnp.arange(half, dtype=jnp.float32) * (2.0 / ROPE_DIM))
    ang = pos.astype(jnp.float32)[:, None] * inv[None, :]
    cos = jnp.cos(ang)[None, :, None, :]
    sin = jnp.sin(ang)[None, :, None, :]
    xf = x.astype(jnp.float32)
    x1 = xf[..., :half]
    x2 = xf[..., half:ROPE_DIM]
    out = jnp.concatenate([x1 * cos - x2 * sin, x2 * cos + x1 * sin, xf[..., ROPE_DIM:]], axis=-1)
    return out.astype(x.dtype)


def pool_mix(ext, n_out, pos_out, w_pool, pool_scale):
    B, L, _ = ext.shape
    P = max(POOL_WINDOWS)
    xf = ext.astype(jnp.float32)
    c = jnp.pad(jnp.cumsum(xf, axis=1), ((0, 0), (P, 0), (0, 0)))
    start = L - n_out
    pos = pos_out + jnp.arange(n_out, dtype=jnp.float32)
    parts = []
    for g, w in enumerate(POOL_WINDOWS):
        lo, hi = g * POOL_GROUP, (g + 1) * POOL_GROUP
        win = c[:, P + start:P + L, lo:hi] - c[:, P + start - w:P + L - w, lo:hi]
        cnt = jnp.minimum(pos + 1.0, jnp.float32(w))
        parts.append(win / cnt[None, :, None] - xf[:, start:, lo:hi])
    d = jnp.stack(parts, axis=2)
    y = jnp.einsum('btgc,gcd->btgd', d, w_pool.astype(jnp.float32)).reshape(B, n_out, D_POOL)
    return (y * pool_scale.astype(jnp.float32)).astype(ext.dtype)


def sink_softmax(s, sink):
    sk = sink.astype(jnp.float32).reshape(N_KV_HEADS, GQA_GROUP)[:, :, None, None]
    m = jnp.maximum(jnp.max(s, axis=-1, keepdims=True), sk)
    e = jnp.exp(s - m)
    return e / (jnp.sum(e, axis=-1, keepdims=True) + jnp.exp(sk - m))


def swa_prompt(q, k, v, sink):
    B, S = q.shape[:2]
    nb = S // BLOCK
    qb = q.reshape(B, nb, BLOCK, N_KV_HEADS, GQA_GROUP, HEAD_DIM)
    kb = k.reshape(B, nb, BLOCK, N_KV_HEADS, HEAD_DIM)
    vb = v.reshape(B, nb, BLOCK, N_KV_HEADS, HEAD_DIM)
    pad = ((0, 0), (1, 0), (0, 0), (0, 0), (0, 0))
    kband = jnp.concatenate([jnp.pad(kb, pad)[:, :-1], kb], axis=2)
    vband = jnp.concatenate([jnp.pad(vb, pad)[:, :-1], vb], axis=2)
    s = jnp.einsum('bnqkgd,bnskd->bnkgqs', qb, kband, preferred_element_type=jnp.float32) * (HEAD_DIM ** -0.5)
    qrel = jnp.arange(BLOCK)[:, None] + BLOCK
    krel = jnp.arange(2 * BLOCK)[None, :]
    diff = qrel - krel
    band = (diff >= 0) & (diff <= WINDOW)
    has_prev = (jnp.arange(nb)[:, None, None] > 0) | (krel[None] >= BLOCK)
    mask = band[None] & has_prev
    s = jnp.where(mask[None, :, None, None], s, -jnp.inf)
    p = sink_softmax(s, sink)
    o = jnp.einsum('bnkgqs,bnskd->bnqkgd', p.astype(v.dtype), vband, preferred_element_type=jnp.float32)
    return o.reshape(B, S, D_ATTN).astype(q.dtype)


def swa_decode(q, k_all, v_all, q_pos, k_pos, sink):
    B, T = q.shape[:2]
    qg = q.reshape(B, T, N_KV_HEADS, GQA_GROUP, HEAD_DIM)
    s = jnp.einsum('btkgd,bskd->bkgts', qg, k_all, preferred_element_type=jnp.float32) * (HEAD_DIM ** -0.5)
    diff = q_pos[:, None] - k_pos[None, :]
    mask = (diff >= 0) & (diff <= WINDOW)
    s = jnp.where(mask, s, -jnp.inf)
    p = sink_softmax(s, sink)
    o = jnp.einsum('bkgts,bskd->btkgd', p.astype(v_all.dtype), v_all, preferred_element_type=jnp.float32)
    return o.reshape(B, T, D_ATTN).astype(q.dtype)


def merge_out(pool_o, g_pool, att_o, g_attn, w_out):
    mixed = jnp.concatenate([pool_o * jax.nn.silu(g_pool), att_o * jax.nn.silu(g_attn)], axis=-1)
    return mixed @ w_out


def setup_inputs(seed: int = 0) -> dict:
    key = jax.random.key(seed)
    ks = jax.random.split(key, 12)
    n_buf = min(WINDOW, PAST_LEN)
    f32 = jnp.float32
    nrm = jax.random.normal
    return {
        'x_prompt': nrm(ks[0], (BATCH, SEQ, D_MODEL), f32),
        'x_sample': nrm(ks[1], (DEC_BATCH, DEC_SEQ, D_MODEL), f32),
        'state_pool': nrm(ks[2], (DEPTH, DEC_BATCH, POOL_STATE, D_POOL), f32),
        'state_k_win': nrm(ks[3], (DEPTH, DEC_BATCH, n_buf, N_KV_HEADS, HEAD_DIM), f32),
        'state_v_win': nrm(ks[4], (DEPTH, DEC_BATCH, n_buf, N_KV_HEADS, HEAD_DIM), f32),
        'norm_g': 1.0 + 0.1 * nrm(ks[5], (DEPTH, D_MODEL), f32),
        'w_in': nrm(ks[6], (DEPTH, D_MODEL, N_IN), f32) * (D_MODEL ** -0.5),
        'w_pool': nrm(ks[7], (DEPTH, N_POOL_GROUPS, POOL_GROUP, POOL_GROUP), f32) * (POOL_GROUP ** -0.5),
        'pool_scale': 1.0 + 0.1 * nrm(ks[8], (DEPTH, D_POOL), f32),
        'attn_sinks': nrm(ks[9], (DEPTH, N_HEADS), f32),
        'w_out': nrm(ks[10], (DEPTH, D_MIX, D_MODEL), f32) * (D_MIX ** -0.5),
        'final_norm_g': 1.0 + 0.1 * nrm(ks[11], (D_MODEL,), f32),
    }


def reference(x_prompt, x_sample, state_pool, state_k_win, state_v_win, norm_g, w_in, w_pool,
              pool_scale, attn_sinks, w_out, final_norm_g):
    B, S = x_prompt.shape[:2]
    DB, T = x_sample.shape[:2]
    n_buf = state_k_win.shape[2]
    pos_p = jnp.arange(S, dtype=jnp.int32)
    pos_s = PAST_LEN + jnp.arange(T, dtype=jnp.int32)
    kpos_s = PAST_LEN - n_buf + jnp.arange(n_buf + T, dtype=jnp.int32)
    xp, xs = x_prompt, x_sample
    pp_pool, pp_k, pp_v, ps_pool, ps_k, ps_v = [], [], [], [], [], []
    for l in range(DEPTH):
        h = rms_norm(xp, norm_g[l])
        u, gp, q, k, v, ga = split_cols(h @ w_in[l])
        pool_o = pool_mix(u, S, 0, w_pool[l], pool_scale[l])
        q = rope_partial(q.reshape(B, S, N_HEADS, HEAD_DIM), pos_p)
        k = rope_partial(k.reshape(B, S, N_KV_HEADS, HEAD_DIM), pos_p)
        v = v.reshape(B, S, N_KV_HEADS, HEAD_DIM)
        att = swa_prompt(q, k, v, attn_sinks[l])
        xp = xp + merge_out(pool_o, gp, att, ga, w_out[l])
        pp_pool.append(u[:, S - POOL_STATE:])
        pp_k.append(k[:, S - WINDOW:])
        pp_v.append(v[:, S - WINDOW:])
        h = rms_norm(xs, norm_g[l])
        u, gp, q, k, v, ga = split_cols(h @ w_in[l])
        ext = jnp.concatenate([state_pool[l].astype(u.dtype), u], axis=1)
        pool_o = pool_mix(ext, T, PAST_LEN, w_pool[l], pool_scale[l])
        q = rope_partial(q.reshape(DB, T, N_HEADS, HEAD_DIM), pos_s)
        k = rope_partial(k.reshape(DB, T, N_KV_HEADS, HEAD_DIM), pos_s)
        v = v.reshape(DB, T, N_KV_HEADS, HEAD_DIM)
        k_all = jnp.concatenate([state_k_win[l].astype(k.dtype), k], axis=1)
        v_all = jnp.concatenate([state_v_win[l].astype(v.dtype), v], axis=1)
        att = swa_decode(q, k_all, v_all, pos_s, kpos_s, attn_sinks[l])
        xs = xs + merge_out(pool_o, gp, att, ga, w_out[l])
        ps_pool.append(ext[:, ext.shape[1] - POOL_STATE:])
        ps_k.append(k_all[:, k_all.shape[1] - n_buf:])
        ps_v.append(v_all[:, v_all.shape[1] - n_buf:])
    y_prompt = rms_norm(xp, final_norm_g)
    y_sample = rms_norm(xs, final_norm_g)
    new_pool_prompt = jnp.stack(pp_pool, axis=0)
    new_k_prompt = jnp.stack(pp_k, axis=0)
    new_v_prompt = jnp.stack(pp_v, axis=0)
    new_pool_sample = jnp.stack(ps_pool, axis=0)
    new_k_sample = jnp.stack(ps_k, axis=0)
    new_v_sample = jnp.stack(ps_v, axis=0)
    return (y_prompt, y_sample, new_pool_prompt, new_k_prompt, new_v_prompt, new_pool_sample, new_k_sample, new_v_sample)
```

```python
import contextlib

import numpy as np

import concourse.bass as bass
import concourse.mybir as mybir
from concourse.bass_utils import run_bass_kernel_spmd

F32 = mybir.dt.float32
BF = mybir.dt.bfloat16
ALU = mybir.AluOpType
AF = mybir.ActivationFunctionType

NCORES = 8
D = 2048
SEQ = 8192
BATCH = 2
DB = 128
CT = 2048
T = 512
NTILE = CT // T
HALO = 128
NS = DB // NCORES
PAST = 16384
EPS = 1e-5
NCH_IN = 34
NCH = 50
WSLOTS = 4
LIMIT = None
PREFETCH = 3

KINDS = ([("ga", i) for i in range(8)] + [("k", 0), ("v", 0)] + [("q", i) for i in range(8)] +
         [("gp", i) for i in range(8)] + [("u", i) for i in (6, 7, 4, 5, 2, 3, 0, 1)] + [("o", i) for i in range(16)])
U_FIRST = next(i for i, k in enumerate(KINDS) if k[0] == "u")


class _Rec:
    def __init__(self):
        self.call = None

    def __getattr__(self, name):
        def f(*a, **k):
            self.call = (name, a, k)
        return f


class Sched:
    ENGS = ("pe", "act", "dve", "pool", "sp")

    def __init__(self, nc):
        self.nc = nc
        self.ins = []
        self.res = {}

    PSUM_PREFIX = ("zb", "stat", "sps", "xps")

    def op(self, eng, fn, reads=(), writes=(), dma=None):
        writes = list(writes) + [r for r in reads if r.startswith(self.PSUM_PREFIX) and r not in writes]
        raw, war = set(), set()
        for r in reads:
            st = self.res.get(r)
            if st is not None and st[0] is not None:
                raw.add(st[0])
        for w in writes:
            st = self.res.get(w)
            if st is not None:
                if st[0] is not None:
                    war.add(st[0])
                war.update(st[1])
        i = len(self.ins)
        if fn is not None:
            rec = _Rec()
            fn(rec)
            name, a, k = rec.call
            fn = lambda e, name=name, a=a, k=k: getattr(e, name)(*a, **k)
        self.ins.append(dict(eng=eng, fn=fn, raw=raw, war=war - raw, dma=dma))
        for r in reads:
            self.res.setdefault(r, [None, []])[1].append(i)
        for w in writes:
            self.res[w] = [i, []]
        return i

    def _deps(self, i):
        ins = self.ins[i]
        out = []
        for d in ins["raw"]:
            dep = self.ins[d]
            if dep["dma"] is None and ins["dma"] is None and dep["eng"] == ins["eng"] == "pe":
                continue
            out.append(d)
        for d in ins["war"]:
            dep = self.ins[d]
            if dep["dma"] is None and ins["dma"] is None and dep["eng"] == ins["eng"] == "pe":
                continue
            out.append(d)
        return out

    def emit(self, limit=None):
        nc = self.nc
        if limit is not None and limit < len(self.ins):
            self.ins = self.ins[:limit]
            alld = [i for i, x in enumerate(self.ins) if x["dma"] is not None]
            self.ins.append(dict(eng="sp", fn=None, raw=set(alld), war=set(), dma=None))
        n = len(self.ins)
        needed = [False] * n
        deps = [self._deps(i) for i in range(n)]
        for i in range(n):
            for d in deps[i]:
                needed[d] = True
        cnt = {e: 0 for e in self.ENGS}
        dcnt = {}
        val = [None] * n
        for i, ins in enumerate(self.ins):
            if ins["dma"] is not None:
                k = "dma:" + ins["dma"]
                dcnt[k] = dcnt.get(k, 0) + 16
                val[i] = (k, dcnt[k])
            elif needed[i]:
                cnt[ins["eng"]] += 1
                val[i] = (ins["eng"], cnt[ins["eng"]])
        keys = [e for e in self.ENGS if cnt[e] > 0] + sorted(dcnt)
        per_eng = {e: [] for e in self.ENGS}
        for i, ins in enumerate(self.ins):
            per_eng[ins["eng"]].append(i)
        self.stats = dict(n=n, sems=len(keys), per_eng={e: len(v) for e, v in per_eng.items()})
        with contextlib.ExitStack() as es:
            sems = {}
            for k in keys:
                sems[k] = es.enter_context(nc.semaphore("s_" + k.replace(":", "_")))
            block = es.enter_context(nc.Block())

            def run(ename, eng):
                waited = {}
                for i in per_eng[ename]:
                    ins = self.ins[i]
                    need = {}
                    for d in deps[i]:
                        k, v = val[d]
                        if v > need.get(k, 0):
                            need[k] = v
                    for k, v in need.items():
                        if waited.get(k, 0) >= v:
                            continue
                        eng.wait_ge(sems[k], v)
                        waited[k] = v
                    if ins["fn"] is None:
                        continue
                    r = ins["fn"](eng)
                    if val[i] is not None:
                        k, v = val[i]
                        r.then_inc(sems[k], 16 if ins["dma"] is not None else 1)

            @block.tensor
            def _(e):
                run("pe", e)

            @block.scalar
            def _(e):
                run("act", e)

            @block.vector
            def _(e):
                run("dve", e)

            @block.gpsimd
            def _(e):
                run("pool", e)

            @block.sync
            def _(e):
                run("sp", e)


class Ring:
    def __init__(self, tiles, prefix):
        self.tiles = tiles
        self.prefix = prefix
        self.i = 0

    def next(self):
        k = self.i % len(self.tiles)
        self.i += 1
        return self.tiles[k], f"{self.prefix}{k}"


class Grp:
    pass


def build_nc():
    nc = bass.Bass("TRN2", target_bir_lowering=False)

    def din(name, shape, dt=F32):
        return nc.dram_tensor(name, shape, dt, kind="ExternalInput").ap()

    def dout(name, shape, dt=F32):
        return nc.dram_tensor(name, shape, dt, kind="ExternalOutput").ap()

    xT = din("xT", [128, 16, HALO + CT])
    xsT = din("xsT", [128, 16, NS])
    w_all = din("w_all", [NCH, 128, 16, 128])
    wpool_d = din("wpool", [128, 4, 2, 256])
    cmat_d = din("cmat", [128, 8, 128])
    cols_d = din("cols", [128, 64])
    icnt_d = din("icnt", [128, 8, 16])
    tabs_d = din("tabs", [128, 2, HALO + CT])
    tabs_s_d = din("tabs_s", [128, 2, NS])
    spT_d = din("spT", [128, 8, NS, 15])
    skT_d = din("skT", [128, NS, 128])
    sv_d = din("sv", [128, NS, 2, 128])
    sp_nat = din("sp_nat", [NS, 15, 1024])
    sk_nat = din("sk_nat", [NS, 128, 128])
    sv_nat = din("sv_nat", [NS, 128, 128])

    yT = dout("yT", [128, 16, CT])
    ysT = dout("ysT", [128, 16, NS])
    upT = dout("upT", [128, 8, 16])
    kT_last = dout("kT_last", [128, 128])
    v_last = dout("v_last", [128, 128])
    o_sp = dout("o_sp", [NS, 14, 1024])
    o_sk = dout("o_sk", [NS, 127, 128])
    o_sv = dout("o_sv", [NS, 127, 128])
    us_new = dout("us_new", [128, 8, NS])
    ks_new = dout("ks_new", [128, NS])
    vs_new = dout("vs_new", [128, NS])

    S = Sched(nc)
    with contextlib.ExitStack() as es:
        def sb(name, shape, dt):
            return es.enter_context(nc.sbuf_tensor(name, shape, dt))

        def ps(name, shape, dt=F32):
            return es.enter_context(nc.psum_tensor(name, shape, dt))

        bfts = Ring([sb(f"bfts{i}", [128, NS], BF) for i in range(4)], "bfts")
        hTs = sb("hTs", [128, 16, NS], BF)
        mixs = sb("mixs", [128, 16, NS], BF)
        qs = sb("qs", [128, NS, 8], BF)
        ksb = sb("ksb", [128, NS], BF)
        skb = sb("skb", [128, NS, 128], BF)
        vas = sb("vas", [128, NS, 2, 128], BF)
        pts = sb("pts", [128, NS, 2, 8], BF)
        ds = sb("ds", [128, 2, NS], BF)
        smb1 = sb("smb1", [128, 8, NS], BF)
        smb2 = sb("smb2", [128, 8, NS], BF)
        hT = sb("hT", [128, 16, T], BF)
        wsl = [sb(f"wsl{i}", [128, 16, 128], BF) for i in range(WSLOTS)]
        wp = sb("wp", [128, 4, 2, 256], BF)
        cmat = sb("cmat_sb", [128, 8, 128], BF)
        dbuf = [sb(f"dbuf{i}", [128, 2, T], BF) for i in range(2)]
        mixed = sb("mixed", [128, 16, T], BF)
        qrot = sb("qrot", [128, 8, T], BF)
        kt = sb("kt", [128, 5, 2, 128], BF)
        va = sb("va", [128, 5, 2, 128], BF)
        bft = Ring([sb(f"bft{i}", [128, T], BF) for i in range(3)], "bft")
        ptr = Ring([sb(f"pt{i}", [128, 2, 2, T], BF) for i in range(3)], "pt")
        hTh = sb("hTh", [128, 16, HALO], BF)
        xts = sb("xts", [128, 16, NS], F32)
        ksf = sb("ksf", [128, NS], F32)
        vsf = sb("vsf", [128, NS], F32)
        us = sb("us", [128, 8, NS], F32)
        spg = sb("spg", [128, 2, NS, 15], F32)
        sm1 = sb("sm1", [128, 8, NS], F32)
        sm2 = sb("sm2", [128, 8, NS], F32)
        sm3 = sb("sm3", [128, 8, NS], F32)
        xt = [sb(f"xt{i}", [128, 16, T], F32) for i in range(2)]
        cols = sb("cols_sb", [128, 64], F32)
        icnt = sb("icnt_sb", [128, 8, 16], F32)
        ps5 = sb("ps5", [128, 8], F32)
        sinke = sb("sinke", [128, 8], F32)
        ubufs = [sb(f"ubuf{i}", [128, 2, 16 + T], F32) for i in range(2)]
        wa = sb("wa", [128, 2, 16 + T], F32)
        wb = sb("wb", [128, 2, 16 + T], F32)
        uh = sb("uh", [128, 8, 16], F32)
        tabs = sb("tabs_sb", [128, 2, T], F32)
        tabs_h = sb("tabs_h", [128, 2, HALO], F32)
        tabs_s = sb("tabs_ssb", [128, 2, NS], F32)
        f32t = Ring([sb(f"f32t{i}", [128, T], F32) for i in range(4)], "f32t")
        rstd = sb("rstd", [128, T], F32)
        f32ts = Ring([sb(f"f32ts{i}", [128, NS], F32) for i in range(6)], "f32ts")
        zr = Ring([ps(f"zb{i}", [128, T]) for i in range(3)], "zb")
        stat = ps("stat", [128, T])
        sr = Ring([ps(f"sps{i}", [128, T]) for i in range(2)], "sps")
        xr = Ring([ps(f"xps{i}", [128, T]) for i in range(2)], "xps")

        ONES, PERM, ONESA, ONESB, HALF, MK0, MK1, MK2 = range(8)
        gcol = lambda kc: cols[:, kc:kc + 1]
        gfcol = lambda n: cols[:, 16 + n:17 + n]

        S.op("pool", lambda e: e.dma_start(out=cmat[:], in_=cmat_d), writes=["cmat"], dma="cmat")
        def load_xt(p):
            b = p % 2
            for q in range(4):
                S.op("sp", lambda e, b=b, q=q, p=p: e.dma_start(
                    out=xt[b][:, 4 * q:4 * q + 4, :],
                    in_=xT[:, 4 * q:4 * q + 4, HALO + p * T:HALO + (p + 1) * T]),
                    writes=[f"xt{b}q{q}"], dma=f"xt{b}q{q}")

        S.op("sp", lambda e: e.dma_start(out=cols[:], in_=cols_d), writes=["cols"], dma="cols")
        load_xt(0)
        S.op("sp", lambda e: e.dma_start(out=icnt[:], in_=icnt_d), writes=["icnt"], dma="icnt")
        S.op("sp", lambda e: e.dma_start(out=tabs_h[:], in_=tabs_d[:, :, 0:HALO]), writes=["tabs_h"], dma="tabs_h")
        S.op("sp", lambda e: e.dma_start(out=tabs_s[:], in_=tabs_s_d), writes=["tabs_s"], dma="tabs_s")

        S.op("sp", lambda e: e.dma_start(out=xt[1][:, :, 0:HALO], in_=xT[:, :, 0:HALO]),
             writes=[f"xt1q{q}" for q in range(4)], dma="xth")
        S.op("pool", lambda e: e.dma_start(out=wp[:], in_=wpool_d), reads=["xt0q3"], writes=["wp"], dma="wp")
        S.op("dve", lambda e: e.tensor_scalar_mul(out=ps5[:], in0=cols[:, 32:40], scalar1=0.5),
             reads=["cols"], writes=["ps5"])
        S.op("act", lambda e: e.activation(out=sinke[:], in_=cols[:, 40:48], func=AF.Exp, bias=0.6931471805599453),
             reads=["cols"], writes=["sinke"])
        S.op("dve", lambda e: e.memset(va[:], 0.0), writes=[f"va{i}" for i in range(5)])
        S.op("dve", lambda e: e.memset(kt[:], 0.0), writes=[f"kt{i}" for i in range(5)])

        wstate = dict(next=0)
        total_chunks = NTILE * NCH

        def issue_w():
            g = wstate["next"]
            if g >= total_chunks:
                return
            wstate["next"] += 1
            ci = g % NCH
            sl = g % WSLOTS
            early = ["xt0q3"] if 0 < g < PREFETCH else []
            S.op("pool", lambda e, ci=ci, sl=sl: e.dma_start(out=wsl[sl][:], in_=w_all[ci]),
                 reads=early, writes=[f"w{sl}"], dma=f"w{sl}")

        for _ in range(PREFETCH):
            issue_w()

        def mk_group(name, n, xt_t, xres, hT_t, mixed_t, qrot_t, tab_t, tabres):
            g = Grp()
            g.f32t, g.bft = (f32ts, bfts) if name == "smp" else (f32t, bft)
            g.name, g.n, g.xt, g.xres, g.hT = name, n, xt_t, xres, hT_t
            g.mixed, g.qrot, g.tab, g.tabres = mixed_t, qrot_t, tab_t, tabres
            return g

        stat_of = {}

        pending_stats = []

        def phaseA_stats(g, kc, defer=0, alt=False):
            n = g.n
            if kc == 0:
                stat_of[g.name] = zr.next() if g.name == "smp" else (stat, "stat")
            st_t, st_r = stat_of[g.name]
            sq, sqr = bft.next()
            if alt and kc % 2 == 1:
                S.op("dve", lambda e: e.tensor_tensor(out=sq[:, :n], in0=g.xt[:, kc, :n], in1=g.xt[:, kc, :n], op=ALU.mult),
                     reads=[g.xres(kc)], writes=[sqr])
            else:
                S.op("act", lambda e: e.activation(out=sq[:, :n], in_=g.xt[:, kc, :n], func=AF.Square),
                     reads=[g.xres(kc)], writes=[sqr])

            def mm():
                S.op("pe", lambda e: e.matmul(st_t[:, :n], lhsT=cmat[:, ONES, :], rhs=sq[:, :n],
                                              start=(kc == 0), stop=(kc == 15)),
                     reads=[sqr, "cmat"], writes=[st_r])
            if defer:
                pending_stats.append(mm)
            else:
                mm()

        def phaseA_rstd(g, from_stat=False):
            n = g.n
            st_t, st_r = (stat, "stat") if from_stat else stat_of.get(g.name, (stat, "stat"))
            ms, msr = f32t.next()
            S.op("dve", lambda e: e.tensor_scalar_add(out=ms[:, :n], in0=st_t[:, :n], scalar1=EPS),
                 reads=[st_r], writes=[msr])
            S.op("act", lambda e: e.activation(out=ms[:, :n], in_=ms[:, :n], func=AF.Ln), reads=[msr], writes=[msr])
            S.op("act", lambda e: e.activation(out=rstd[:, :n], in_=ms[:, :n], func=AF.Exp, scale=-0.5),
                 reads=[msr], writes=["rstd"])

        def phaseA_scale(g, kc):
            n = g.n
            S.op("dve", lambda e: e.scalar_tensor_tensor(
                out=g.hT[:, kc, :n], in0=g.xt[:, kc, :n], scalar=gcol(kc), in1=rstd[:, :n],
                op0=ALU.mult, op1=ALU.mult),
                reads=[g.xres(kc), "rstd", "cols"], writes=[f"hT_{g.name}"])

        def phaseA(g, alt=False):
            n = g.n
            for kc in range(16):
                phaseA_stats(g, kc, alt=alt)
            phaseA_rstd(g)
            for kc in range(16):
                if False and alt and kc % 2 == 1:
                    tmp, tmpr = f32t.next()
                    S.op("act", lambda e: e.activation(out=tmp[:, :n], in_=g.xt[:, kc, :n], func=AF.Copy, scale=gcol(kc)),
                         reads=[g.xres(kc), "cols"], writes=[tmpr])
                    S.op("pool", lambda e: e.tensor_tensor(out=g.hT[:, kc, :n], in0=tmp[:, :n], in1=rstd[:, :n], op=ALU.mult),
                         reads=[tmpr, "rstd"], writes=[f"hT_{g.name}"])
                else:
                    phaseA_scale(g, kc)

        deferred = []

        def tick():
            keep = []
            for item in deferred:
                item[0] -= 1
                if item[0] <= 0:
                    item[1]()
                else:
                    keep.append(item)
            deferred[:] = keep

        def flush():
            while deferred:
                tick()

        def gate_evac(g, zb, zres, dst, dres):
            n = g.n
            th, thr = g.f32t.next()
            S.op("act", lambda e: e.activation(out=th[:, :n], in_=zb[:, :n], func=AF.Tanh, scale=0.5),
                 reads=[zres], writes=[thr])
            S.op("dve", lambda e: e.scalar_tensor_tensor(out=dst, in0=th[:, :n], scalar=1.0, in1=zb[:, :n],
                                                         op0=ALU.add, op1=ALU.mult),
                 reads=[thr, zres], writes=[dres])

        def rope_evac(g, zb, zres, final):
            n = g.n
            t1, t1r = g.f32t.next()
            zc, zcr = g.bft.next()
            S.op("dve", lambda e: e.tensor_tensor(out=t1[:, :n], in0=zb[:, :n], in1=g.tab[:, 0, :n], op=ALU.mult),
                 reads=[zres, g.tabres], writes=[t1r])
            S.op("act", lambda e: e.activation(func=AF.Copy, out=zc[:, :n], in_=zb[:, :n]), reads=[zres], writes=[zcr])

            def later():
                sw, swr = zr.next()
                t2, t2r = g.f32t.next()
                S.op("pe", lambda e: e.matmul(sw[:, :n], lhsT=cmat[:, PERM, :], rhs=zc[:, :n], start=True, stop=True),
                     reads=[zcr, "cmat"], writes=[swr])
                S.op("dve", lambda e: e.tensor_tensor(out=t2[:, :n], in0=sw[:, :n], in1=g.tab[:, 1, :n], op=ALU.mult),
                     reads=[swr, g.tabres], writes=[t2r])
                final(t1, t1r, t2, t2r)
            deferred.append([2, later])

        def evac(g, kind, idx, zb, zres, p):
            n = g.n
            if kind == "gp":
                gate_evac(g, zb, zres, g.mixed[:, idx, :n], f"mix_{g.name}{idx}")
            elif kind == "ga":
                gate_evac(g, zb, zres, g.mixed[:, 8 + idx, :n], f"mix_{g.name}{8 + idx}")
            elif kind == "q":
                def fin(t1, t1r, t2, t2r):
                    qdst = g.qrot[:, idx, :n] if g.name == "main" else qs[:, :, idx]
                    S.op("pool", lambda e: e.tensor_tensor(out=qdst, in0=t1[:, :n], in1=t2[:, :n], op=ALU.add),
                         reads=[t1r, t2r], writes=[f"q_{g.name}{idx}"])
                rope_evac(g, zb, zres, fin)
            elif kind == "k":
                def fin(t1, t1r, t2, t2r):
                    S.op("pool", lambda e: e.tensor_tensor(out=t1[:, :n], in0=t1[:, :n], in1=t2[:, :n], op=ALU.add),
                         reads=[t1r, t2r], writes=[t1r])
                    if g.name == "main":
                        for j in range(2):
                            S.op("pool", lambda e: e.tensor_copy(
                                out=kt[64 * j:64 * j + 64, 1:5, j, :],
                                in_=t1[64 * j:64 * j + 64, :].rearrange("p (a b) -> p a b", a=4)),
                                reads=[t1r], writes=[f"kt{i}" for i in range(1, 5)])
                        if p == NTILE - 1:
                            S.op("sp", lambda e: e.dma_start(out=kT_last, in_=t1[:, T - 128:T]), reads=[t1r],
                                 writes=["o_kT"], dma="o_kT")
                    elif g.name == "halo":
                        for j in range(2):
                            S.op("pool", lambda e: e.tensor_copy(out=kt[64 * j:64 * j + 64, 0, j, :],
                                                                 in_=t1[64 * j:64 * j + 64, :HALO]),
                                 reads=[t1r], writes=["kt0"])
                    else:
                        S.op("pool", lambda e: e.tensor_copy(out=ksf[:], in_=t1[:, :NS]), reads=[t1r], writes=["ksf"])
                        S.op("pool", lambda e: e.tensor_copy(out=ksb[:], in_=t1[:, :NS]), reads=[t1r], writes=["ksb"])
                        S.op("sp", lambda e: e.dma_start(out=ks_new, in_=ksf[:]), reads=["ksf"], writes=["o_ksn"], dma="o_ksn")
                rope_evac(g, zb, zres, fin)
            elif kind == "v":
                S.op("act", lambda e: e.activation(func=AF.Copy, out=vsf[:], in_=zb[:, :NS]), reads=[zres], writes=["vsf"])
                S.op("sp", lambda e: e.dma_start(out=vs_new, in_=vsf[:]), reads=["vsf"], writes=["o_vsn"], dma="o_vsn")
            elif kind == "u":
                if g.name == "main":
                    j = idx % 2
                    ub = (idx // 2) % 2
                    ubuf = ubufs[ub]
                    S.op("act", lambda e: e.activation(out=ubuf[:, j, 0:16], in_=uh[:, idx, :], func=AF.Copy),
                         reads=[f"uh{idx}"], writes=[f"ubuf{ub}_{j}"])
                    S.op("act", lambda e: e.activation(out=ubuf[:, j, 16:16 + T], in_=zb[:, :], func=AF.Copy),
                         reads=[zres], writes=[f"ubuf{ub}_{j}"])
                    S.op("act", lambda e: e.activation(out=uh[:, idx, :], in_=ubuf[:, j, T:T + 16], func=AF.Copy),
                         reads=[f"ubuf{ub}_{j}"], writes=[f"uh{idx}"])
                    if j == 1:
                        pool_math(idx // 2, p)
                elif g.name == "halo":
                    S.op("act", lambda e: e.activation(out=uh[:, idx, :], in_=zb[:, HALO - 16:HALO], func=AF.Copy),
                         reads=[zres], writes=[f"uh{idx}"])
                else:
                    S.op("act", lambda e: e.activation(func=AF.Copy, out=us[:, idx, :], in_=zb[:, :NS]), reads=[zres], writes=[f"us{idx}"])
                    if idx % 2 == 1:
                        pool_math_s(idx // 2)
            elif kind == "o":
                q = idx // 4
                xr_ = g.xres(idx)
                S.op("dve", lambda e: e.tensor_tensor(out=g.xt[:, idx, :n], in0=zb[:, :n], in1=g.xt[:, idx, :n], op=ALU.add),
                     reads=[zres, xr_], writes=[xr_])
                if g.name == "smp":
                    return
                sq, sqr = bft.next()
                S.op("act", lambda e: e.activation(out=sq[:, :n], in_=g.xt[:, idx, :n], func=AF.Square),
                     reads=[xr_], writes=[sqr])

                def later():
                    S.op("pe", lambda e: e.matmul(stat[:, :n], lhsT=cmat[:, ONES, :], rhs=sq[:, :n],
                                                  start=(idx == 0), stop=(idx == 15)),
                         reads=[sqr, "cmat"], writes=["stat"])
                deferred.append([2, later])

        def pool_mm(g, gi, dsrc, dres):
            n = g.n
            for mc in range(2):
                zb, zres = zr.next()
                for kc in range(2):
                    S.op("pe", lambda e, kc=kc, zb=zb: e.matmul(
                        zb[:, :n], lhsT=wp[:, gi, kc, mc * 128:(mc + 1) * 128], rhs=dsrc(kc),
                        start=(kc == 0), stop=(kc == 1)), reads=[dres, "wp"], writes=[zres])
                ch = 2 * gi + mc
                S.op("dve", lambda e, zb=zb, ch=ch: e.scalar_tensor_tensor(
                    out=g.mixed[:, ch, :n], in0=zb[:, :n], scalar=ps5[:, ch:ch + 1], in1=g.mixed[:, ch, :n],
                    op0=ALU.mult, op1=ALU.mult), reads=[zres, "ps5", f"mix_{g.name}{ch}"], writes=[f"mix_{g.name}{ch}"])

        def pool_math(gi, p):
            L = 16 + T
            ub = gi % 2
            ubuf = ubufs[ub]
            ur = [f"ubuf{ub}_0", f"ubuf{ub}_1"]
            src, srcr = ubuf, ur
            tmps = [(wa, ["wa"]), (wb, ["wb"])]
            sh = 1
            for step in range(gi + 1):
                dst, dstr = tmps[step % 2]
                lo = 2 * sh - 1
                S.op("pool", lambda e, dst=dst, src=src, lo=lo, sh=sh: e.tensor_tensor(
                    out=dst[:, :, lo:L], in0=src[:, :, lo:L], in1=src[:, :, lo - sh:L - sh], op=ALU.add),
                    reads=srcr, writes=dstr)
                src, srcr = dst, dstr
                sh *= 2
            w = float(2 ** (gi + 1))
            db = dbuf[gi % 2]
            dres = f"dbuf{gi % 2}"
            S.op("dve", lambda e, src=src: e.scalar_tensor_tensor(
                out=db[:, :, :], in0=src[:, :, 16:L], scalar=1.0 / w, in1=ubuf[:, :, 16:L],
                op0=ALU.mult, op1=ALU.subtract), reads=srcr + ur, writes=[dres])
            if p == 0:
                tmp, tmpr = f32t.next()
                tv = tmp[:, 0:32].rearrange("p (a b) -> p a b", a=2)
                S.op("pool", lambda e, src=src: e.tensor_tensor(out=tv, in0=src[:, :, 16:32],
                                                               in1=icnt[:, 2 * gi:2 * gi + 2, :], op=ALU.mult),
                     reads=srcr + ["icnt"], writes=[tmpr])
                S.op("pool", lambda e: e.tensor_tensor(out=db[:, :, 0:16], in0=tv, in1=ubuf[:, :, 16:32], op=ALU.subtract),
                     reads=[tmpr] + ur, writes=[dres])
            deferred.append([4, lambda: pool_mm(gmain, gi, lambda kc: db[:, kc, :], dres)])

        def sample_window_sums():
            for gi in range(4):
                w = 2 ** (gi + 1)
                S.op("sp", lambda e: e.dma_start(out=spg[:], in_=spT_d[:, 2 * gi:2 * gi + 2, :, :]),
                     writes=["spg"], dma="spg")
                S.op("dve", lambda e: e.tensor_reduce(out=sm3[:, 2 * gi:2 * gi + 2, :], in_=spg[:, :, :, 16 - w:15],
                                                      axis=mybir.AxisListType.X, op=ALU.add),
                     reads=["spg"], writes=["sm3"])

        def pool_math_s(gi):
            w = 2 ** (gi + 1)
            tmp, tmpr = f32t.next()
            tv = tmp[:, 0:2 * NS].rearrange("p (a b) -> p a b", a=2)
            usv = us[:, 2 * gi:2 * gi + 2, :]
            urs = [f"us{2 * gi}", f"us{2 * gi + 1}"]
            S.op("dve", lambda e: e.tensor_tensor(out=tv, in0=sm3[:, 2 * gi:2 * gi + 2, :], in1=usv, op=ALU.add),
                 reads=["sm3"] + urs, writes=[tmpr])
            S.op("dve", lambda e: e.scalar_tensor_tensor(out=ds[:], in0=tv, scalar=1.0 / w, in1=usv,
                                                         op0=ALU.mult, op1=ALU.subtract),
                 reads=[tmpr] + urs, writes=["ds"])
            deferred.append([2, lambda: pool_mm(gsmp, gi, lambda kc: ds[:, kc, :], "ds")])

        att = {}

        def att_S(p, i):
            blk, hh = divmod(i, 2)
            pt, ptres = ptr.next()
            att[i] = (pt, ptres)
            k = 0
            for j in range(2):
                for kb in range(2):
                    slot = blk + kb
                    sbk, sres = sr.next()
                    S.op("pe", lambda e: e.matmul(
                        sbk[:, :], lhsT=kt[:, slot, j, :],
                        rhs=qrot[:, 4 * hh:4 * hh + 4, blk * 128:(blk + 1) * 128],
                        start=True, stop=True),
                        reads=[f"kt{slot}"] + [f"q_main{c}" for c in range(4 * hh, 4 * hh + 4)], writes=[sres])
                    S.op("act", lambda e: e.activation(out=pt[:, j, kb, :], in_=sbk[:, :], func=AF.Exp, scale=0.125),
                         reads=[sres], writes=[f"{ptres}_{j}{kb}"])
                    mi = MK2 if kb == 1 else (MK0 if (p == 0 and blk == 0) else MK1)
                    pv = pt[:, j, kb, :].rearrange("p (a b) -> p a b", a=4)
                    S.op("pool" if k % 4 != 3 else "dve", lambda e: e.tensor_tensor(
                        out=pv, in0=pv, in1=cmat[:, mi:mi + 1, :].to_broadcast([128, 4, 128]), op=ALU.mult),
                        reads=[f"{ptres}_{j}{kb}", "cmat"], writes=[f"{ptres}_{j}{kb}"])
                    k += 1

        def att_X(p, i):
            blk, hh = divmod(i, 2)
            pt, ptres = att.pop(i)
            xb, xres = xr.next()
            dbk, dres = xr.next()
            k = 0
            for j in range(2):
                for kb in range(2):
                    slot = blk + kb
                    S.op("pe", lambda e: e.matmul(xb[:, :], lhsT=va[:, slot, j, :], rhs=pt[:, j, kb, :],
                                                  start=(k == 0), stop=(k == 3)),
                         reads=[f"va{slot}", f"{ptres}_{j}{kb}"], writes=[xres])
                    k += 1
            k = 0
            for j in range(2):
                for kb in range(2):
                    S.op("pe", lambda e: e.matmul(dbk[:, :], lhsT=cmat[:, ONESA + j, :], rhs=pt[:, j, kb, :],
                                                  start=(k == 0), stop=(k == 3)),
                         reads=["cmat", f"{ptres}_{j}{kb}"], writes=[dres])
                    k += 1
            dp, dpr = f32t.next()
            dpv = dp[:, :].rearrange("p (a b) -> p a b", a=4)
            S.op("dve", lambda e: e.tensor_tensor(
                out=dpv, in0=dbk[:, :].rearrange("p (a b) -> p a b", a=4),
                in1=sinke[:, 4 * hh:4 * hh + 4].unsqueeze(2).to_broadcast([128, 4, 128]), op=ALU.add),
                reads=[dres, "sinke"], writes=[dpr])
            S.op("dve", lambda e: e.reciprocal(out=dp[:, :], in_=dp[:, :]), reads=[dpr], writes=[dpr])
            S.op("dve", lambda e: e.tensor_tensor(out=dp[:, :], in0=xb[:, :], in1=dp[:, :], op=ALU.mult),
                 reads=[xres, dpr], writes=[dpr])
            mv = mixed[:, 8 + 4 * hh:12 + 4 * hh, blk * 128:(blk + 1) * 128]
            mres = [f"mix_main{c}" for c in range(8 + 4 * hh, 12 + 4 * hh)]
            S.op("pool", lambda e: e.tensor_tensor(out=mv, in0=dpv, in1=mv, op=ALU.mult),
                 reads=[dpr] + mres, writes=mres)

        def att_roll():
            S.op("pool", lambda e: e.tensor_copy(out=kt[:, 0, :, :], in_=kt[:, 4, :, :]), reads=["kt4"], writes=["kt0"])
            S.op("pool", lambda e: e.tensor_copy(out=va[:, 0, :, :], in_=va[:, 4, :, :]), reads=["va4"], writes=["va0"])

        def attention_s():
            S.op("pool", lambda e: e.dma_start(out=skb[:], in_=skT_d), writes=["skb"], dma="skb")
            S.op("pool", lambda e: e.dma_start(out=vas[:], in_=sv_d), writes=["vas0", "vas1"], dma="vas0")
            qres = [f"q_smp{c}" for c in range(8)]
            sbk, sres = sr.next()
            sv4 = sbk[:, 0:NS * 16].rearrange("p (b j c) -> p b j c", b=NS, j=2)
            for b in range(NS):
                for j in range(2):
                    S.op("pe", lambda e, b=b, j=j: e.matmul(sv4[:, b, j, :], lhsT=skb[64 * j:64 * j + 64, b, :],
                                                           rhs=qs[64 * j:64 * j + 64, b, :], start=True, stop=True),
                         reads=["skb"] + qres, writes=[sres])
            S.op("act", lambda e: e.activation(out=pts[:].rearrange("p b j c -> p (b j c)"), in_=sbk[:, 0:NS * 16],
                                               func=AF.Exp, scale=0.125), reads=[sres], writes=["pts"])
            xb, xres = xr.next()
            dbk, dres = xr.next()
            xv = xb[:, 0:NS * 8].rearrange("p (b c) -> p b c", b=NS)
            dv = dbk[:, 0:NS * 8].rearrange("p (b c) -> p b c", b=NS)
            for b in range(NS):
                for j in range(2):
                    S.op("pe", lambda e, b=b, j=j: e.matmul(xv[:, b, :], lhsT=vas[:, b, j, :], rhs=pts[:, b, j, :],
                                                           start=(j == 0), stop=(j == 1)),
                         reads=["vas0", "vas1", "pts"], writes=[xres])
            for b in range(NS):
                for j in range(2):
                    S.op("pe", lambda e, b=b, j=j: e.matmul(dv[:, b, :], lhsT=cmat[:, ONESA + j, :], rhs=pts[:, b, j, :],
                                                           start=(j == 0), stop=(j == 1)),
                         reads=["cmat", "pts"], writes=[dres])
            S.op("dve", lambda e: e.tensor_tensor(out=sm1[:], in0=qs[:].rearrange("p b c -> p c b"), in1=ksb[:].unsqueeze(1).to_broadcast([128, 8, NS]),
                                                  op=ALU.mult), reads=qres + ["ksb"], writes=["sm1"])
            S.op("act", lambda e: e.activation(func=AF.Copy, out=smb1[:], in_=sm1[:]), reads=["sm1"], writes=["smb1"])
            S.op("dve", lambda e: e.tensor_tensor(out=sm2[:], in0=sm1[:], in1=smb1[:], op=ALU.subtract),
                 reads=["sm1", "smb1"], writes=["sm2"])
            S.op("act", lambda e: e.activation(func=AF.Copy, out=smb2[:], in_=sm2[:]), reads=["sm2"], writes=["smb2"])
            sn, snres = sr.next()
            snv = sn[:, 0:8 * NS]
            S.op("pe", lambda e: e.matmul(snv, lhsT=cmat[:, HALF, :], rhs=smb1[:].rearrange("p c b -> p (c b)"),
                                          start=True, stop=False), reads=["cmat", "smb1"], writes=[snres])
            S.op("pe", lambda e: e.matmul(snv, lhsT=cmat[:, HALF, :], rhs=smb2[:].rearrange("p c b -> p (c b)"),
                                          start=False, stop=True), reads=["cmat", "smb2"], writes=[snres])
            S.op("act", lambda e: e.activation(out=sm1[:].rearrange("p c b -> p (c b)"), in_=snv, func=AF.Exp, scale=0.125),
                 reads=[snres], writes=["sm1"])
            S.op("dve", lambda e: e.scalar_tensor_tensor(out=sm2[:], in0=sm1[:], scalar=2.0, in1=dv.rearrange("p b c -> p c b"),
                                                         op0=ALU.mult, op1=ALU.add),
                 reads=[dres, "sm1"], writes=["sm2"])
            S.op("dve", lambda e: e.tensor_tensor(out=sm2[:], in0=sm2[:], in1=sinke[:].unsqueeze(2).to_broadcast([128, 8, NS]),
                                                  op=ALU.add), reads=["sm2", "sinke"], writes=["sm2"])
            S.op("dve", lambda e: e.reciprocal(out=sm2[:], in_=sm2[:]), reads=["sm2"], writes=["sm2"])
            S.op("dve", lambda e: e.tensor_tensor(out=sm3[:], in0=sm1[:], in1=vsf[:].unsqueeze(1).to_broadcast([128, 8, NS]),
                                                  op=ALU.mult), reads=["sm1", "vsf"], writes=["sm3"])
            S.op("dve", lambda e: e.tensor_tensor(out=sm3[:], in0=xv.rearrange("p b c -> p c b"), in1=sm3[:], op=ALU.add),
                 reads=[xres, "sm3"], writes=["sm3"])
            S.op("dve", lambda e: e.tensor_tensor(out=sm3[:], in0=sm3[:], in1=sm2[:], op=ALU.mult),
                 reads=["sm3", "sm2"], writes=["sm3"])
            mres = [f"mix_smp{c}" for c in range(8, 16)]
            S.op("pool", lambda e: e.tensor_tensor(out=mixs[:, 8:16, :], in0=sm3[:], in1=mixs[:, 8:16, :], op=ALU.mult),
                 reads=["sm3"] + mres, writes=mres)

        def finish_out(g, p):
            n = g.n
            if g.name == "smp":
                for nn in range(16):
                    sq, sqr = bft.next()
                    S.op("act", lambda e, sq=sq, nn=nn: e.activation(out=sq[:, :n], in_=g.xt[:, nn, :n], func=AF.Square),
                         reads=[g.xres(nn)], writes=[sqr])
                    S.op("pe", lambda e, sq=sq, nn=nn: e.matmul(stat[:, :n], lhsT=cmat[:, ONES, :], rhs=sq[:, :n],
                                                                 start=(nn == 0), stop=(nn == 15)),
                         reads=[sqr, "cmat"], writes=["stat"])
            phaseA_rstd(g, from_stat=True)
            for nn in range(16):
                finish_scale(g, p, nn)

        def finish_scale(g, p, nn):
            n = g.n
            S.op("dve", lambda e: e.scalar_tensor_tensor(
                out=g.xt[:, nn, :n], in0=g.xt[:, nn, :n], scalar=gfcol(nn), in1=rstd[:, :n],
                op0=ALU.mult, op1=ALU.mult), reads=[g.xres(nn), "rstd", "cols"], writes=[g.xres(nn)])
            if g.name == "main":
                if nn % 4 == 3:
                    b = p % 2
                    q = nn // 4
                    S.op("sp", lambda e: e.dma_start(out=yT[:, 4 * q:4 * q + 4, p * T:(p + 1) * T],
                                                     in_=xt[b][:, 4 * q:4 * q + 4, :]),
                         reads=[f"xt{b}q{q}"], writes=[f"yo{p}q{q}"], dma=f"yo{b}q{q}")
            elif nn == 15:
                S.op("sp", lambda e: e.dma_start(out=ysT, in_=xts[:]), reads=["xts"], writes=["o_ys"], dma="o_ys")

        ghalo = mk_group("halo", HALO, xt[1], lambda kc: f"xt1q{kc // 4}", hTh, None, None, tabs_h, "tabs_h")
        gsmp = mk_group("smp", NS, xts, lambda kc: "xts", hTs, mixs, qs, tabs_s, "tabs_s")
        gmain = None
        out_res = []

        def state_copies():
            S.op("sp", lambda e: e.dma_start(out=o_sp, in_=sp_nat[:, 1:15, :]), writes=["o_sp"], dma="o_sp")
            S.op("sp", lambda e: e.dma_start(out=o_sk, in_=sk_nat[:, 1:128, :]), writes=["o_sk"], dma="o_sk")
            S.op("sp", lambda e: e.dma_start(out=o_sv, in_=sv_nat[:, 1:128, :]), writes=["o_sv"], dma="o_sv")
        out_res += ["o_sp", "o_sk", "o_sv"]

        def load_tabs(p):
            S.op("sp", lambda e: e.dma_start(out=tabs[:], in_=tabs_d[:, :, HALO + p * T:HALO + (p + 1) * T]),
                 writes=["tabs"], dma="tabs")

        pending_finish = []
        gmains = []
        for p in range(NTILE):
            b = p % 2
            gmains.append(mk_group("main", T, xt[b], lambda kc, b=b: f"xt{b}q{kc // 4}", hT, mixed, qrot, tabs, "tabs"))
        phaseA(gmains[0], alt=True)

        for p in range(NTILE):
            gmain = gmains[p]
            if p == 0:
                load_tabs(0)
            groups = [gmain]
            if p == 0:
                groups = [ghalo, gmain]
            if p + 1 < NTILE and not pending_finish and p > 0:
                load_xt(p + 1)
            if p == NTILE - 1:
                S.op("sp", lambda e: e.dma_start(out=xts[:], in_=xsT), writes=["xts"], dma="xts")
                phaseA(gsmp)
                groups.append(gsmp)

            for ci in range(NCH):
                kind, idx = KINDS[ci]
                if kind == "gp" and idx == 0:
                    flush()
                    if p + 1 < NTILE:
                        load_tabs(p + 1)
                    att_S(p, 0)
                if kind == "o" and idx == 0:
                    flush()
                    att_roll()
                    if p == NTILE - 1:
                        attention_s()
                    if p + 1 < NTILE:
                        while pending_stats:
                            pending_stats.pop(0)()
                        phaseA_rstd(gmains[p + 1])
                issue_w()
                if p == 0 and ci == 4:
                    phaseA(ghalo, alt=True)
                    load_xt(1)
                if p == 0 and ci == 6:
                    sample_window_sums()
                if pending_finish:
                    pg, pp = pending_finish[0]
                    if ci == 2:
                        phaseA_rstd(pg, from_stat=True)
                    if 2 <= ci < 18:
                        finish_scale(pg, pp, ci - 2)
                    if ci == 17:
                        pending_finish.pop()
                        if p + 1 < NTILE:
                            load_xt(p + 1)
                sl = (p * NCH + ci) % WSLOTS
                wres = f"w{sl}"
                for g in groups:
                    n = g.n
                    if g.name == "halo" and kind not in ("u", "k", "v"):
                        continue
                    if kind == "v" and g.name != "smp":
                        nb = n // 128
                        zb, zres = zr.next()
                        for blk in range(nb):
                            for kc in range(16):
                                S.op("pe", lambda e: e.matmul(
                                    zb[:, blk * 128:(blk + 1) * 128], lhsT=g.hT[:, kc, blk * 128:(blk + 1) * 128],
                                    rhs=wsl[sl][:, kc, :], start=(kc == 0), stop=(kc == 15)),
                                    reads=[wres, f"hT_{g.name}"], writes=[zres])
                        s0 = 1 if g.name == "main" else 0
                        zv = zb[:, 0:nb * 128].rearrange("p (a b) -> p a b", a=nb)
                        vres = [f"va{i}" for i in range(s0, s0 + nb)]
                        S.op("act", lambda e: e.activation(func=AF.Copy, out=va[:, s0:s0 + nb, 0, 0:64], in_=zv[:, :, 0:64]),
                             reads=[zres], writes=vres)
                        S.op("dve", lambda e: e.tensor_copy(out=va[:, s0:s0 + nb, 1, 64:128], in_=zv[:, :, 64:128]),
                             reads=[zres], writes=vres)
                        if g.name == "main" and p == NTILE - 1:
                            vf, vfr = f32t.next()
                            S.op("act", lambda e: e.activation(func=AF.Copy, out=vf[:, 0:128], in_=zb[:, T - 128:T]),
                                 reads=[zres], writes=[vfr])
                            S.op("sp", lambda e: e.dma_start(out=v_last, in_=vf[:, 0:128]), reads=[vfr], writes=["o_v"], dma="o_v")
                        continue
                    src = g.hT if ci < NCH_IN else g.mixed
                    if ci < NCH_IN:
                        sres = [f"hT_{g.name}"]
                    else:
                        sres = [f"mix_{g.name}{c}" for c in range(16)]
                    zb, zres = zr.next()
                    for kc in range(16):
                        S.op("pe", lambda e: e.matmul(
                            zb[:, :n], lhsT=wsl[sl][:, kc, :], rhs=src[:, kc, :n], start=(kc == 0), stop=(kc == 15)),
                            reads=[wres] + sres, writes=[zres])
                    evac(g, kind, idx, zb, zres, p)
                if kind == "gp":
                    if idx + 1 < 8:
                        att_S(p, idx + 1)
                    att_X(p, idx)
                if p + 1 < NTILE:
                    if kind == "u":
                        while pending_stats:
                            pending_stats.pop(0)()
                        phaseA_stats(gmains[p + 1], 2 * (ci - U_FIRST), defer=1)
                        phaseA_stats(gmains[p + 1], 2 * (ci - U_FIRST) + 1, defer=1)
                    if kind == "o":
                        phaseA_scale(gmains[p + 1], idx)
                tick()
            flush()
            if p == 0:
                state_copies()
            out_res += [f"yo{p}q{q}" for q in range(4)]
            if p < NTILE - 1:
                pending_finish.append((gmain, p))
            else:
                finish_out(gmain, p)
            if p == NTILE - 1:
                finish_out(gsmp, p)
                out_res += ["o_ys", "o_ksn", "o_vsn", "o_kT", "o_v"]
                S.op("sp", lambda e: e.dma_start(out=upT, in_=uh[:]), reads=[f"uh{i}" for i in range(8)],
                     writes=["o_up"], dma="o_up")
                S.op("sp", lambda e: e.dma_start(out=us_new, in_=us[:]), reads=[f"us{i}" for i in range(8)],
                     writes=["o_usn"], dma="o_usn")
                out_res += ["o_up", "o_usn"]
        S.op("sp", None, reads=out_res)
        S.emit(LIMIT)
    return nc, S.stats


def _weight_layout(w_in, w_out):
    cols = np.zeros((NCH_IN, 128), np.int64)
    pp = np.arange(128)
    hp = np.where(pp < 64, 0, 8)
    dd = pp % 64
    for ci in range(NCH_IN):
        kind, i = KINDS[ci]
        if kind == "gp":
            cols[ci] = 1024 + 128 * i + pp
        elif kind == "u":
            cols[ci] = 128 * i + pp
        elif kind == "ga":
            cols[ci] = 3328 + (i + hp) * 64 + dd
        elif kind == "k":
            cols[ci] = 3072 + pp
        elif kind == "v":
            cols[ci] = 3200 + pp
        elif kind == "q":
            cols[ci] = 2048 + (i + hp) * 64 + dd
    wi = w_in[0][:, cols.reshape(-1)].reshape(16, 128, NCH_IN, 128).transpose(2, 1, 0, 3)
    rows = np.zeros((16, 128), np.int64)
    for kc in range(8):
        rows[kc] = kc * 128 + pp
    for c in range(8):
        rows[8 + c] = 1024 + (c + hp) * 64 + dd
    wo = w_out[0][rows.reshape(-1), :].reshape(16, 128, 16, 128).transpose(2, 1, 0, 3)
    return np.ascontiguousarray(np.concatenate([wi, wo], axis=0), dtype=np.float32)


def _rope_tables(pos):
    half = 8
    inv = np.power(np.float32(500000.0), -np.arange(half, dtype=np.float32) * np.float32(2.0 / 16)).astype(np.float32)
    ang = pos.astype(np.float32)[:, None] * inv[None, :]
    c = np.cos(ang).astype(np.float32).T
    s = np.sin(ang).astype(np.float32).T
    tab = np.zeros((128, 2, pos.shape[0]), np.float32)
    tab[:, 0, :] = 1.0
    for h in range(2):
        base = 64 * h
        tab[base:base + 8, 0] = c
        tab[base + 8:base + 16, 0] = c
        tab[base:base + 8, 1] = -s
        tab[base + 8:base + 16, 1] = s
    return tab


def _const_mats(first_block_has_prev):
    cm = np.zeros((128, 8, 128), np.float32)
    cm[:, 0, :] = 1.0 / 2048.0
    for m in range(128):
        d = m % 64
        if d < 8:
            cm[m + 8, 1, m] = 1.0
        elif d < 16:
            cm[m - 8, 1, m] = 1.0
    cm[:, 2, 0:64] = 2.0
    cm[:, 3, 64:128] = 2.0
    cm[0:64, 4, 0:64] = 1.0
    cm[64:128, 4, 64:128] = 1.0
    s = np.arange(128)[:, None]
    q = np.arange(128)[None, :]
    prev = (q <= s).astype(np.float32)
    cur = (q >= s).astype(np.float32)
    cm[:, 5, :] = prev if first_block_has_prev else 0.0
    cm[:, 6, :] = prev
    cm[:, 7, :] = cur
    return cm


def _pad_v(svn):
    out = np.zeros((128, NS, 2, 128), np.float32)
    t = svn.transpose(1, 0, 2)
    out[:, :, 0, 0:64] = t[:, :, 0:64]
    out[:, :, 1, 64:128] = t[:, :, 64:128]
    return out


_CACHE = {}


def make_in_maps(x_prompt, x_sample, state_pool, state_k_win, state_v_win, norm_g, w_in, w_pool,
                 pool_scale, attn_sinks, w_out, final_norm_g):
    f = lambda a: np.ascontiguousarray(np.asarray(a), dtype=np.float32)
    x_prompt, x_sample = f(x_prompt), f(x_sample)
    state_pool, state_k_win, state_v_win = f(state_pool), f(state_k_win), f(state_v_win)
    norm_g, w_in, w_pool, pool_scale = f(norm_g), f(w_in), f(w_pool), f(pool_scale)
    attn_sinks, w_out, final_norm_g = f(attn_sinks), f(w_out), f(final_norm_g)

    w_all = _weight_layout(w_in, w_out)
    wpool = np.ascontiguousarray(w_pool[0].reshape(4, 2, 128, 256).transpose(2, 0, 1, 3))
    pp = np.arange(128)
    cols = np.zeros((128, 64), np.float32)
    cols[:, 0:16] = norm_g[0].reshape(16, 128).T
    cols[:, 16:32] = final_norm_g.reshape(16, 128).T
    cols[:, 32:40] = pool_scale[0].reshape(8, 128).T
    sk = attn_sinks[0]
    for c in range(8):
        cols[0:64, 40 + c] = sk[c]
        cols[64:128, 40 + c] = sk[8 + c]
    tabs_s = _rope_tables(np.full((NS,), PAST, np.int64))

    in_maps = []
    for c in range(NCORES):
        b, qd = c // 4, c % 4
        t0 = qd * CT
        xx = np.zeros((HALO + CT, D), np.float32)
        if t0 > 0:
            xx[:HALO] = x_prompt[b, t0 - HALO:t0]
        xx[HALO:] = x_prompt[b, t0:t0 + CT]
        xT = np.ascontiguousarray(xx.reshape(HALO + CT, 16, 128).transpose(2, 1, 0))
        xs = x_sample[c * NS:(c + 1) * NS, 0, :]
        xsT = np.ascontiguousarray(xs.reshape(NS, 16, 128).transpose(2, 1, 0))
        pos = np.arange(t0 - HALO, t0 + CT)
        tabs = _rope_tables(np.maximum(pos, 0))
        icnt = np.zeros((128, 8, 16), np.float32)
        for gi, w in enumerate((2, 4, 8, 16)):
            cnt = np.minimum(t0 + np.arange(16) + 1.0, float(w)).astype(np.float32)
            icnt[:, 2 * gi:2 * gi + 2, :] = (np.float32(1.0) / cnt)[None, None, :]
        spn = state_pool[0, c * NS:(c + 1) * NS]
        skn = state_k_win[0, c * NS:(c + 1) * NS].reshape(NS, 128, 128)
        svn = state_v_win[0, c * NS:(c + 1) * NS].reshape(NS, 128, 128)
        in_maps.append({
            "xT": xT, "xsT": xsT, "w_all": w_all, "wpool": wpool,
            "cmat": _const_mats(t0 > 0), "cols": cols, "icnt": icnt,
            "tabs": tabs, "tabs_s": tabs_s,
            "spT": np.ascontiguousarray(spn.reshape(NS, 15, 8, 128).transpose(3, 2, 0, 1)),
            "skT": np.ascontiguousarray(skn.transpose(2, 0, 1)),
            "sv": _pad_v(svn),
            "sp_nat": np.ascontiguousarray(spn), "sk_nat": np.ascontiguousarray(skn),
            "sv_nat": np.ascontiguousarray(svn),
        })

    return in_maps


def kernel(**inputs):
    in_maps = make_in_maps(**inputs)
    if "nc" not in _CACHE:
        _CACHE["nc"] = build_nc()
    nc, _ = _CACHE["nc"]
    res = run_bass_kernel_spmd(nc, in_maps, core_ids=list(range(NCORES)))
    return assemble(res.results)


def assemble(R):
    y_prompt = np.zeros((BATCH, SEQ, D), np.float32)
    y_sample = np.zeros((DB, 1, D), np.float32)
    new_pool_prompt = np.zeros((1, BATCH, 15, 1024), np.float32)
    new_k_prompt = np.zeros((1, BATCH, 128, 2, 64), np.float32)
    new_v_prompt = np.zeros((1, BATCH, 128, 2, 64), np.float32)
    new_pool_sample = np.zeros((1, DB, 15, 1024), np.float32)
    new_k_sample = np.zeros((1, DB, 128, 2, 64), np.float32)
    new_v_sample = np.zeros((1, DB, 128, 2, 64), np.float32)
    for c in range(NCORES):
        b, qd = c // 4, c % 4
        r = R[c]
        y_prompt[b, qd * CT:(qd + 1) * CT] = r["yT"].transpose(2, 1, 0).reshape(CT, D)
        y_sample[c * NS:(c + 1) * NS, 0] = r["ysT"].transpose(2, 1, 0).reshape(NS, D)
        if qd == 3:
            new_pool_prompt[0, b] = r["upT"].transpose(2, 1, 0).reshape(16, 1024)[1:]
            new_k_prompt[0, b] = r["kT_last"].T.reshape(128, 2, 64)
            new_v_prompt[0, b] = r["v_last"].reshape(128, 2, 64)
        sl = slice(c * NS, (c + 1) * NS)
        new_pool_sample[0, sl, :14] = r["o_sp"]
        new_pool_sample[0, sl, 14] = r["us_new"].transpose(2, 1, 0).reshape(NS, 1024)
        new_k_sample[0, sl, :127] = r["o_sk"].reshape(NS, 127, 2, 64)
        new_k_sample[0, sl, 127] = r["ks_new"].T.reshape(NS, 2, 64)
        new_v_sample[0, sl, :127] = r["o_sv"].reshape(NS, 127, 2, 64)
        new_v_sample[0, sl, 127] = r["vs_new"].T.reshape(NS, 2, 64)
    return (y_prompt, y_sample, new_pool_prompt, new_k_prompt, new_v_prompt,
            new_pool_sample, new_k_sample, new_v_sample)
```

```python
import contextlib

import numpy as np

import concourse.bass as bass
import concourse.mybir as mybir
from concourse.bass_utils import run_bass_kernel_spmd

F32 = mybir.dt.float32
BF = mybir.dt.bfloat16
ALU = mybir.AluOpType
AF = mybir.ActivationFunctionType

NCORES = 8
D = 2048
SEQ = 8192
BATCH = 2
DB = 128
CT = 2048
T = 512
NTILE = CT // T
HALO = 128
NS = DB // NCORES
PAST = 16384
EPS = 1e-5
NCH_IN = 34
NCH = 50
WSLOTS = 4
LIMIT = None
PREFETCH = 3

KINDS = ([("ga", i) for i in range(8)] + [("k", 0), ("v", 0)] + [("q", i) for i in range(8)] +
         [("gp", i) for i in range(8)] + [("u", i) for i in (6, 7, 4, 5, 2, 3, 0, 1)] + [("o", i) for i in range(16)])
U_FIRST = next(i for i, k in enumerate(KINDS) if k[0] == "u")


class _Rec:
    def __init__(self):
        self.call = None

    def __getattr__(self, name):
        def f(*a, **k):
            self.call = (name, a, k)
        return f


class Sched:
    ENGS = ("pe", "act", "dve", "pool", "sp")

    def __init__(self, nc):
        self.nc = nc
        self.ins = []
        self.res = {}

    PSUM_PREFIX = ("zb", "stat", "sps", "xps")

    def op(self, eng, fn, reads=(), writes=(), dma=None):
        writes = list(writes) + [r for r in reads if r.startswith(self.PSUM_PREFIX) and r not in writes]
        raw, war = set(), set()
        for r in reads:
            st = self.res.get(r)
            if st is not None and st[0] is not None:
                raw.add(st[0])
        for w in writes:
            st = self.res.get(w)
            if st is not None:
                if st[0] is not None:
                    war.add(st[0])
                war.update(st[1])
        i = len(self.ins)
        if fn is not None:
            rec = _Rec()
            fn(rec)
            name, a, k = rec.call
            fn = lambda e, name=name, a=a, k=k: getattr(e, name)(*a, **k)
        self.ins.append(dict(eng=eng, fn=fn, raw=raw, war=war - raw, dma=dma))
        for r in reads:
            self.res.setdefault(r, [None, []])[1].append(i)
        for w in writes:
            self.res[w] = [i, []]
        return i

    def _deps(self, i):
        ins = self.ins[i]
        out = []
        for d in ins["raw"]:
            dep = self.ins[d]
            if dep["dma"] is None and ins["dma"] is None and dep["eng"] == ins["eng"] == "pe":
                continue
            out.append(d)
        for d in ins["war"]:
            dep = self.ins[d]
            if dep["dma"] is None and ins["dma"] is None and dep["eng"] == ins["eng"] == "pe":
                continue
            out.append(d)
        return out

    def emit(self, limit=None):
        nc = self.nc
        if limit is not None and limit < len(self.ins):
            self.ins = self.ins[:limit]
            alld = [i for i, x in enumerate(self.ins) if x["dma"] is not None]
            self.ins.append(dict(eng="sp", fn=None, raw=set(alld), war=set(), dma=None))
        n = len(self.ins)
        needed = [False] * n
        deps = [self._deps(i) for i in range(n)]
        for i in range(n):
            for d in deps[i]:
                needed[d] = True
        cnt = {e: 0 for e in self.ENGS}
        dcnt = {}
        val = [None] * n
        for i, ins in enumerate(self.ins):
            if ins["dma"] is not None:
                k = "dma:" + ins["dma"]
                dcnt[k] = dcnt.get(k, 0) + 16
                val[i] = (k, dcnt[k])
            elif needed[i]:
                cnt[ins["eng"]] += 1
                val[i] = (ins["eng"], cnt[ins["eng"]])
        keys = [e for e in self.ENGS if cnt[e] > 0] + sorted(dcnt)
        per_eng = {e: [] for e in self.ENGS}
        for i, ins in enumerate(self.ins):
            per_eng[ins["eng"]].append(i)
        self.stats = dict(n=n, sems=len(keys), per_eng={e: len(v) for e, v in per_eng.items()})
        with contextlib.ExitStack() as es:
            sems = {}
            for k in keys:
                sems[k] = es.enter_context(nc.semaphore("s_" + k.replace(":", "_")))
            block = es.enter_context(nc.Block())

            def run(ename, eng):
                waited = {}
                for i in per_eng[ename]:
                    ins = self.ins[i]
                    need = {}
                    for d in deps[i]:
                        k, v = val[d]
                        if v > need.get(k, 0):
                            need[k] = v
                    for k, v in need.items():
                        if waited.get(k, 0) >= v:
                            continue
                        eng.wait_ge(sems[k], v)
                        waited[k] = v
                    if ins["fn"] is None:
                        continue
                    r = ins["fn"](eng)
                    if val[i] is not None:
                        k, v = val[i]
                        r.then_inc(sems[k], 16 if ins["dma"] is not None else 1)

            @block.tensor
            def _(e):
                run("pe", e)

            @block.scalar
            def _(e):
                run("act", e)

            @block.vector
            def _(e):
                run("dve", e)

            @block.gpsimd
            def _(e):
                run("pool", e)

            @block.sync
            def _(e):
                run("sp", e)


class Ring:
    def __init__(self, tiles, prefix):
        self.tiles = tiles
        self.prefix = prefix
        self.i = 0

    def next(self):
        k = self.i % len(self.tiles)
        self.i += 1
        return self.tiles[k], f"{self.prefix}{k}"


class Grp:
    pass


def build_nc():
    nc = bass.Bass("TRN2", target_bir_lowering=False)

    def din(name, shape, dt=F32):
        return nc.dram_tensor(name, shape, dt, kind="ExternalInput").ap()

    def dout(name, shape, dt=F32):
        return nc.dram_tensor(name, shape, dt, kind="ExternalOutput").ap()

    xT = din("xT", [128, 16, HALO + CT])
    xsT = din("xsT", [128, 16, NS])
    w_all = din("w_all", [NCH, 128, 16, 128])
    wpool_d = din("wpool", [128, 4, 2, 256])
    cmat_d = din("cmat", [128, 8, 128])
    cols_d = din("cols", [128, 64])
    icnt_d = din("icnt", [128, 8, 16])
    tabs_d = din("tabs", [128, 2, HALO + CT])
    tabs_s_d = din("tabs_s", [128, 2, NS])
    spT_d = din("spT", [128, 8, NS, 15])
    skT_d = din("skT", [128, NS, 128])
    sv_d = din("sv", [128, NS, 2, 128])
    sp_nat = din("sp_nat", [NS, 15, 1024])
    sk_nat = din("sk_nat", [NS, 128, 128])
    sv_nat = din("sv_nat", [NS, 128, 128])

    yT = dout("yT", [128, 16, CT])
    ysT = dout("ysT", [128, 16, NS])
    upT = dout("upT", [128, 8, 16])
    kT_last = dout("kT_last", [128, 128])
    v_last = dout("v_last", [128, 128])
    o_sp = dout("o_sp", [NS, 14, 1024])
    o_sk = dout("o_sk", [NS, 127, 128])
    o_sv = dout("o_sv", [NS, 127, 128])
    us_new = dout("us_new", [128, 8, NS])
    ks_new = dout("ks_new", [128, NS])
    vs_new = dout("vs_new", [128, NS])

    S = Sched(nc)
    with contextlib.ExitStack() as es:
        def sb(name, shape, dt):
            return es.enter_context(nc.sbuf_tensor(name, shape, dt))

        def ps(name, shape, dt=F32):
            return es.enter_context(nc.psum_tensor(name, shape, dt))

        bfts = Ring([sb(f"bfts{i}", [128, NS], BF) for i in range(4)], "bfts")
        hTs = sb("hTs", [128, 16, NS], BF)
        mixs = sb("mixs", [128, 16, NS], BF)
        qs = sb("qs", [128, NS, 8], BF)
        ksb = sb("ksb", [128, NS], BF)
        skb = sb("skb", [128, NS, 128], BF)
        vas = sb("vas", [128, NS, 2, 128], BF)
        pts = sb("pts", [128, NS, 2, 8], BF)
        ds = sb("ds", [128, 2, NS], BF)
        smb1 = sb("smb1", [128, 8, NS], BF)
        smb2 = sb("smb2", [128, 8, NS], BF)
        hT = sb("hT", [128, 16, T], BF)
        wsl = [sb(f"wsl{i}", [128, 16, 128], BF) for i in range(WSLOTS)]
        wp = sb("wp", [128, 4, 2, 256], BF)
        cmat = sb("cmat_sb", [128, 8, 128], BF)
        dbuf = [sb(f"dbuf{i}", [128, 2, T], BF) for i in range(2)]
        mixed = sb("mixed", [128, 16, T], BF)
        qrot = sb("qrot", [128, 8, T], BF)
        kt = sb("kt", [128, 5, 2, 128], BF)
        va = sb("va", [128, 5, 2, 128], BF)
        bft = Ring([sb(f"bft{i}", [128, T], BF) for i in range(3)], "bft")
        ptr = Ring([sb(f"pt{i}", [128, 2, 2, T], BF) for i in range(3)], "pt")
        hTh = sb("hTh", [128, 16, HALO], BF)
        xts = sb("xts", [128, 16, NS], F32)
        ksf = sb("ksf", [128, NS], F32)
        vsf = sb("vsf", [128, NS], F32)
        us = sb("us", [128, 8, NS], F32)
        spg = sb("spg", [128, 2, NS, 15], F32)
        sm1 = sb("sm1", [128, 8, NS], F32)
        sm2 = sb("sm2", [128, 8, NS], F32)
        sm3 = sb("sm3", [128, 8, NS], F32)
        xt = [sb(f"xt{i}", [128, 16, T], F32) for i in range(2)]
        cols = sb("cols_sb", [128, 64], F32)
        icnt = sb("icnt_sb", [128, 8, 16], F32)
        ps5 = sb("ps5", [128, 8], F32)
        sinke = sb("sinke", [128, 8], F32)
        ubufs = [sb(f"ubuf{i}", [128, 2, 16 + T], F32) for i in range(2)]
        wa = sb("wa", [128, 2, 16 + T], F32)
        wb = sb("wb", [128, 2, 16 + T], F32)
        uh = sb("uh", [128, 8, 16], F32)
        tabs = sb("tabs_sb", [128, 2, T], F32)
        tabs_h = sb("tabs_h", [128, 2, HALO], F32)
        tabs_s = sb("tabs_ssb", [128, 2, NS], F32)
        f32t = Ring([sb(f"f32t{i}", [128, T], F32) for i in range(4)], "f32t")
        rstd = sb("rstd", [128, T], F32)
        f32ts = Ring([sb(f"f32ts{i}", [128, NS], F32) for i in range(6)], "f32ts")
        zr = Ring([ps(f"zb{i}", [128, T]) for i in range(3)], "zb")
        stat = ps("stat", [128, T])
        sr = Ring([ps(f"sps{i}", [128, T]) for i in range(2)], "sps")
        xr = Ring([ps(f"xps{i}", [128, T]) for i in range(2)], "xps")

        ONES, PERM, ONESA, ONESB, HALF, MK0, MK1, MK2 = range(8)
        gcol = lambda kc: cols[:, kc:kc + 1]
        gfcol = lambda n: cols[:, 16 + n:17 + n]

        S.op("pool", lambda e: e.dma_start(out=cmat[:], in_=cmat_d), writes=["cmat"], dma="cmat")
        def load_xt(p):
            b = p % 2
            for q in range(4):
                S.op("sp", lambda e, b=b, q=q, p=p: e.dma_start(
                    out=xt[b][:, 4 * q:4 * q + 4, :],
                    in_=xT[:, 4 * q:4 * q + 4, HALO + p * T:HALO + (p + 1) * T]),
                    writes=[f"xt{b}q{q}"], dma=f"xt{b}q{q}")

        S.op("sp", lambda e: e.dma_start(out=cols[:], in_=cols_d), writes=["cols"], dma="cols")
        load_xt(0)
        S.op("sp", lambda e: e.dma_start(out=icnt[:], in_=icnt_d), writes=["icnt"], dma="icnt")
        S.op("sp", lambda e: e.dma_start(out=tabs_h[:], in_=tabs_d[:, :, 0:HALO]), writes=["tabs_h"], dma="tabs_h")
        S.op("sp", lambda e: e.dma_start(out=tabs_s[:], in_=tabs_s_d), writes=["tabs_s"], dma="tabs_s")

        S.op("sp", lambda e: e.dma_start(out=xt[1][:, :, 0:HALO], in_=xT[:, :, 0:HALO]),
             writes=[f"xt1q{q}" for q in range(4)], dma="xth")
        S.op("pool", lambda e: e.dma_start(out=wp[:], in_=wpool_d), reads=["xt0q3"], writes=["wp"], dma="wp")
        S.op("dve", lambda e: e.tensor_scalar_mul(out=ps5[:], in0=cols[:, 32:40], scalar1=0.5),
             reads=["cols"], writes=["ps5"])
        S.op("act", lambda e: e.activation(out=sinke[:], in_=cols[:, 40:48], func=AF.Exp, bias=0.6931471805599453),
             reads=["cols"], writes=["sinke"])
        S.op("dve", lambda e: e.memset(va[:], 0.0), writes=[f"va{i}" for i in range(5)])
        S.op("dve", lambda e: e.memset(kt[:], 0.0), writes=[f"kt{i}" for i in range(5)])

        wstate = dict(next=0)
        total_chunks = NTILE * NCH

        def issue_w():
            g = wstate["next"]
            if g >= total_chunks:
                return
            wstate["next"] += 1
            ci = g % NCH
            sl = g % WSLOTS
            early = ["xt0q3"] if 0 < g < PREFETCH else []
            S.op("pool", lambda e, ci=ci, sl=sl: e.dma_start(out=wsl[sl][:], in_=w_all[ci]),
                 reads=early, writes=[f"w{sl}"], dma=f"w{sl}")

        for _ in range(PREFETCH):
            issue_w()

        def mk_group(name, n, xt_t, xres, hT_t, mixed_t, qrot_t, tab_t, tabres):
            g = Grp()
            g.f32t, g.bft = (f32ts, bfts) if name == "smp" else (f32t, bft)
            g.name, g.n, g.xt, g.xres, g.hT = name, n, xt_t, xres, hT_t
            g.mixed, g.qrot, g.tab, g.tabres = mixed_t, qrot_t, tab_t, tabres
            return g

        stat_of = {}

        pending_stats = []

        def phaseA_stats(g, kc, defer=0, alt=False):
            n = g.n
            if kc == 0:
                stat_of[g.name] = zr.next() if g.name == "smp" else (stat, "stat")
            st_t, st_r = stat_of[g.name]
            sq, sqr = bft.next()
            if alt and kc % 2 == 1:
                S.op("dve", lambda e: e.tensor_tensor(out=sq[:, :n], in0=g.xt[:, kc, :n], in1=g.xt[:, kc, :n], op=ALU.mult),
                     reads=[g.xres(kc)], writes=[sqr])
            else:
                S.op("act", lambda e: e.activation(out=sq[:, :n], in_=g.xt[:, kc, :n], func=AF.Square),
                     reads=[g.xres(kc)], writes=[sqr])

            def mm():
                S.op("pe", lambda e: e.matmul(st_t[:, :n], lhsT=cmat[:, ONES, :], rhs=sq[:, :n],
                                              start=(kc == 0), stop=(kc == 15)),
                     reads=[sqr, "cmat"], writes=[st_r])
            if defer:
                pending_stats.append(mm)
            else:
                mm()

        def phaseA_rstd(g, from_stat=False):
            n = g.n
            st_t, st_r = (stat, "stat") if from_stat else stat_of.get(g.name, (stat, "stat"))
            ms, msr = f32t.next()
            S.op("dve", lambda e: e.tensor_scalar_add(out=ms[:, :n], in0=st_t[:, :n], scalar1=EPS),
                 reads=[st_r], writes=[msr])
            S.op("act", lambda e: e.activation(out=ms[:, :n], in_=ms[:, :n], func=AF.Ln), reads=[msr], writes=[msr])
            S.op("act", lambda e: e.activation(out=rstd[:, :n], in_=ms[:, :n], func=AF.Exp, scale=-0.5),
                 reads=[msr], writes=["rstd"])

        def phaseA_scale(g, kc):
            n = g.n
            S.op("dve", lambda e: e.scalar_tensor_tensor(
                out=g.hT[:, kc, :n], in0=g.xt[:, kc, :n], scalar=gcol(kc), in1=rstd[:, :n],
                op0=ALU.mult, op1=ALU.mult),
                reads=[g.xres(kc), "rstd", "cols"], writes=[f"hT_{g.name}"])

        def phaseA(g, alt=False):
            n = g.n
            for kc in range(16):
                phaseA_stats(g, kc, alt=alt)
            phaseA_rstd(g)
            for kc in range(16):
                if False and alt and kc % 2 == 1:
                    tmp, tmpr = f32t.next()
                    S.op("act", lambda e: e.activation(out=tmp[:, :n], in_=g.xt[:, kc, :n], func=AF.Copy, scale=gcol(kc)),
                         reads=[g.xres(kc), "cols"], writes=[tmpr])
                    S.op("pool", lambda e: e.tensor_tensor(out=g.hT[:, kc, :n], in0=tmp[:, :n], in1=rstd[:, :n], op=ALU.mult),
                         reads=[tmpr, "rstd"], writes=[f"hT_{g.name}"])
                else:
                    phaseA_scale(g, kc)

        deferred = []

        def tick():
            keep = []
            for item in deferred:
                item[0] -= 1
                if item[0] <= 0:
                    item[1]()
                else:
                    keep.append(item)
            deferred[:] = keep

        def flush():
            while deferred:
                tick()

        def gate_evac(g, zb, zres, dst, dres):
            n = g.n
            th, thr = g.f32t.next()
            S.op("act", lambda e: e.activation(out=th[:, :n], in_=zb[:, :n], func=AF.Tanh, scale=0.5),
                 reads=[zres], writes=[thr])
            S.op("dve", lambda e: e.scalar_tensor_tensor(out=dst, in0=th[:, :n], scalar=1.0, in1=zb[:, :n],
                                                         op0=ALU.add, op1=ALU.mult),
                 reads=[thr, zres], writes=[dres])

        def rope_evac(g, zb, zres, final):
            n = g.n
            t1, t1r = g.f32t.next()
            zc, zcr = g.bft.next()
            S.op("dve", lambda e: e.tensor_tensor(out=t1[:, :n], in0=zb[:, :n], in1=g.tab[:, 0, :n], op=ALU.mult),
                 reads=[zres, g.tabres], writes=[t1r])
            S.op("act", lambda e: e.activation(func=AF.Copy, out=zc[:, :n], in_=zb[:, :n]), reads=[zres], writes=[zcr])

            def later():
                sw, swr = zr.next()
                t2, t2r = g.f32t.next()
                S.op("pe", lambda e: e.matmul(sw[:, :n], lhsT=cmat[:, PERM, :], rhs=zc[:, :n], start=True, stop=True),
                     reads=[zcr, "cmat"], writes=[swr])
                S.op("dve", lambda e: e.tensor_tensor(out=t2[:, :n], in0=sw[:, :n], in1=g.tab[:, 1, :n], op=ALU.mult),
                     reads=[swr, g.tabres], writes=[t2r])
                final(t1, t1r, t2, t2r)
            deferred.append([2, later])

        def evac(g, kind, idx, zb, zres, p):
            n = g.n
            if kind == "gp":
                gate_evac(g, zb, zres, g.mixed[:, idx, :n], f"mix_{g.name}{idx}")
            elif kind == "ga":
                gate_evac(g, zb, zres, g.mixed[:, 8 + idx, :n], f"mix_{g.name}{8 + idx}")
            elif kind == "q":
                def fin(t1, t1r, t2, t2r):
                    qdst = g.qrot[:, idx, :n] if g.name == "main" else qs[:, :, idx]
                    S.op("pool", lambda e: e.tensor_tensor(out=qdst, in0=t1[:, :n], in1=t2[:, :n], op=ALU.add),
                         reads=[t1r, t2r], writes=[f"q_{g.name}{idx}"])
                rope_evac(g, zb, zres, fin)
            elif kind == "k":
                def fin(t1, t1r, t2, t2r):
                    S.op("pool", lambda e: e.tensor_tensor(out=t1[:, :n], in0=t1[:, :n], in1=t2[:, :n], op=ALU.add),
                         reads=[t1r, t2r], writes=[t1r])
                    if g.name == "main":
                        for j in range(2):
                            S.op("pool", lambda e: e.tensor_copy(
                                out=kt[64 * j:64 * j + 64, 1:5, j, :],
                                in_=t1[64 * j:64 * j + 64, :].rearrange("p (a b) -> p a b", a=4)),
                                reads=[t1r], writes=[f"kt{i}" for i in range(1, 5)])
                        if p == NTILE - 1:
                            S.op("sp", lambda e: e.dma_start(out=kT_last, in_=t1[:, T - 128:T]), reads=[t1r],
                                 writes=["o_kT"], dma="o_kT")
                    elif g.name == "halo":
                        for j in range(2):
                            S.op("pool", lambda e: e.tensor_copy(out=kt[64 * j:64 * j + 64, 0, j, :],
                                                                 in_=t1[64 * j:64 * j + 64, :HALO]),
                                 reads=[t1r], writes=["kt0"])
                    else:
                        S.op("pool", lambda e: e.tensor_copy(out=ksf[:], in_=t1[:, :NS]), reads=[t1r], writes=["ksf"])
                        S.op("pool", lambda e: e.tensor_copy(out=ksb[:], in_=t1[:, :NS]), reads=[t1r], writes=["ksb"])
                        S.op("sp", lambda e: e.dma_start(out=ks_new, in_=ksf[:]), reads=["ksf"], writes=["o_ksn"], dma="o_ksn")
                rope_evac(g, zb, zres, fin)
            elif kind == "v":
                S.op("act", lambda e: e.activation(func=AF.Copy, out=vsf[:], in_=zb[:, :NS]), reads=[zres], writes=["vsf"])
                S.op("sp", lambda e: e.dma_start(out=vs_new, in_=vsf[:]), reads=["vsf"], writes=["o_vsn"], dma="o_vsn")
            elif kind == "u":
                if g.name == "main":
                    j = idx % 2
                    ub = (idx // 2) % 2
                    ubuf = ubufs[ub]
                    S.op("act", lambda e: e.activation(out=ubuf[:, j, 0:16], in_=uh[:, idx, :], func=AF.Copy),
                         reads=[f"uh{idx}"], writes=[f"ubuf{ub}_{j}"])
                    S.op("act", lambda e: e.activation(out=ubuf[:, j, 16:16 + T], in_=zb[:, :], func=AF.Copy),
                         reads=[zres], writes=[f"ubuf{ub}_{j}"])
                    S.op("act", lambda e: e.activation(out=uh[:, idx, :], in_=ubuf[:, j, T:T + 16], func=AF.Copy),
                         reads=[f"ubuf{ub}_{j}"], writes=[f"uh{idx}"])
                    if j == 1:
                        pool_math(idx // 2, p)
                elif g.name == "halo":
                    S.op("act", lambda e: e.activation(out=uh[:, idx, :], in_=zb[:, HALO - 16:HALO], func=AF.Copy),
                         reads=[zres], writes=[f"uh{idx}"])
                else:
                    S.op("act", lambda e: e.activation(func=AF.Copy, out=us[:, idx, :], in_=zb[:, :NS]), reads=[zres], writes=[f"us{idx}"])
                    if idx % 2 == 1:
                        pool_math_s(idx // 2)
            elif kind == "o":
                q = idx // 4
                xr_ = g.xres(idx)
                S.op("dve", lambda e: e.tensor_tensor(out=g.xt[:, idx, :n], in0=zb[:, :n], in1=g.xt[:, idx, :n], op=ALU.add),
                     reads=[zres, xr_], writes=[xr_])
                if g.name == "smp":
                    return
                sq, sqr = bft.next()
                S.op("act", lambda e: e.activation(out=sq[:, :n], in_=g.xt[:, idx, :n], func=AF.Square),
                     reads=[xr_], writes=[sqr])

                def later():
                    S.op("pe", lambda e: e.matmul(stat[:, :n], lhsT=cmat[:, ONES, :], rhs=sq[:, :n],
                                                  start=(idx == 0), stop=(idx == 15)),
                         reads=[sqr, "cmat"], writes=["stat"])
                deferred.append([2, later])

        def pool_mm(g, gi, dsrc, dres):
            n = g.n
            for mc in range(2):
                zb, zres = zr.next()
                for kc in range(2):
                    S.op("pe", lambda e, kc=kc, zb=zb: e.matmul(
                        zb[:, :n], lhsT=wp[:, gi, kc, mc * 128:(mc + 1) * 128], rhs=dsrc(kc),
                        start=(kc == 0), stop=(kc == 1)), reads=[dres, "wp"], writes=[zres])
                ch = 2 * gi + mc
                S.op("dve", lambda e, zb=zb, ch=ch: e.scalar_tensor_tensor(
                    out=g.mixed[:, ch, :n], in0=zb[:, :n], scalar=ps5[:, ch:ch + 1], in1=g.mixed[:, ch, :n],
                    op0=ALU.mult, op1=ALU.mult), reads=[zres, "ps5", f"mix_{g.name}{ch}"], writes=[f"mix_{g.name}{ch}"])

        def pool_math(gi, p):
            L = 16 + T
            ub = gi % 2
            ubuf = ubufs[ub]
            ur = [f"ubuf{ub}_0", f"ubuf{ub}_1"]
            src, srcr = ubuf, ur
            tmps = [(wa, ["wa"]), (wb, ["wb"])]
            sh = 1
            for step in range(gi + 1):
                dst, dstr = tmps[step % 2]
                lo = 2 * sh - 1
                S.op("pool", lambda e, dst=dst, src=src, lo=lo, sh=sh: e.tensor_tensor(
                    out=dst[:, :, lo:L], in0=src[:, :, lo:L], in1=src[:, :, lo - sh:L - sh], op=ALU.add),
                    reads=srcr, writes=dstr)
                src, srcr = dst, dstr
                sh *= 2
            w = float(2 ** (gi + 1))
            db = dbuf[gi % 2]
            dres = f"dbuf{gi % 2}"
            S.op("dve", lambda e, src=src: e.scalar_tensor_tensor(
                out=db[:, :, :], in0=src[:, :, 16:L], scalar=1.0 / w, in1=ubuf[:, :, 16:L],
                op0=ALU.mult, op1=ALU.subtract), reads=srcr + ur, writes=[dres])
            if p == 0:
                tmp, tmpr = f32t.next()
                tv = tmp[:, 0:32].rearrange("p (a b) -> p a b", a=2)
                S.op("pool", lambda e, src=src: e.tensor_tensor(out=tv, in0=src[:, :, 16:32],
                                                               in1=icnt[:, 2 * gi:2 * gi + 2, :], op=ALU.mult),
                     reads=srcr + ["icnt"], writes=[tmpr])
                S.op("pool", lambda e: e.tensor_tensor(out=db[:, :, 0:16], in0=tv, in1=ubuf[:, :, 16:32], op=ALU.subtract),
                     reads=[tmpr] + ur, writes=[dres])
            deferred.append([4, lambda: pool_mm(gmain, gi, lambda kc: db[:, kc, :], dres)])

        def sample_window_sums():
            for gi in range(4):
                w = 2 ** (gi + 1)
                S.op("sp", lambda e: e.dma_start(out=spg[:], in_=spT_d[:, 2 * gi:2 * gi + 2, :, :]),
                     writes=["spg"], dma="spg")
                S.op("dve", lambda e: e.tensor_reduce(out=sm3[:, 2 * gi:2 * gi + 2, :], in_=spg[:, :, :, 16 - w:15],
                                                      axis=mybir.AxisListType.X, op=ALU.add),
                     reads=["spg"], writes=["sm3"])

        def pool_math_s(gi):
            w = 2 ** (gi + 1)
            tmp, tmpr = f32t.next()
            tv = tmp[:, 0:2 * NS].rearrange("p (a b) -> p a b", a=2)
            usv = us[:, 2 * gi:2 * gi + 2, :]
            urs = [f"us{2 * gi}", f"us{2 * gi + 1}"]
            S.op("dve", lambda e: e.tensor_tensor(out=tv, in0=sm3[:, 2 * gi:2 * gi + 2, :], in1=usv, op=ALU.add),
                 reads=["sm3"] + urs, writes=[tmpr])
            S.op("dve", lambda e: e.scalar_tensor_tensor(out=ds[:], in0=tv, scalar=1.0 / w, in1=usv,
                                                         op0=ALU.mult, op1=ALU.subtract),
                 reads=[tmpr] + urs, writes=["ds"])
            deferred.append([2, lambda: pool_mm(gsmp, gi, lambda kc: ds[:, kc, :], "ds")])

        att = {}

        def att_S(p, i):
            blk, hh = divmod(i, 2)
            pt, ptres = ptr.next()
            att[i] = (pt, ptres)
            k = 0
            for j in range(2):
                for kb in range(2):
                    slot = blk + kb
                    sbk, sres = sr.next()
                    S.op("pe", lambda e: e.matmul(
                        sbk[:, :], lhsT=kt[:, slot, j, :],
                        rhs=qrot[:, 4 * hh:4 * hh + 4, blk * 128:(blk + 1) * 128],
                        start=True, stop=True),
                        reads=[f"kt{slot}"] + [f"q_main{c}" for c in range(4 * hh, 4 * hh + 4)], writes=[sres])
                    S.op("act", lambda e: e.activation(out=pt[:, j, kb, :], in_=sbk[:, :], func=AF.Exp, scale=0.125),
                         reads=[sres], writes=[f"{ptres}_{j}{kb}"])
                    mi = MK2 if kb == 1 else (MK0 if (p == 0 and blk == 0) else MK1)
                    pv = pt[:, j, kb, :].rearrange("p (a b) -> p a b", a=4)
                    S.op("pool" if k % 4 == 0 else "dve", lambda e: e.tensor_tensor(
                        out=pv, in0=pv, in1=cmat[:, mi:mi + 1, :].to_broadcast([128, 4, 128]), op=ALU.mult),
                        reads=[f"{ptres}_{j}{kb}", "cmat"], writes=[f"{ptres}_{j}{kb}"])
                    k += 1

        def att_X(p, i):
            blk, hh = divmod(i, 2)
            pt, ptres = att.pop(i)
            xb, xres = xr.next()
            dbk, dres = xr.next()
            k = 0
            for j in range(2):
                for kb in range(2):
                    slot = blk + kb
                    S.op("pe", lambda e: e.matmul(xb[:, :], lhsT=va[:, slot, j, :], rhs=pt[:, j, kb, :],
                                                  start=(k == 0), stop=(k == 3)),
                         reads=[f"va{slot}", f"{ptres}_{j}{kb}"], writes=[xres])
                    k += 1
            k = 0
            for j in range(2):
                for kb in range(2):
                    S.op("pe", lambda e: e.matmul(dbk[:, :], lhsT=cmat[:, ONESA + j, :], rhs=pt[:, j, kb, :],
                                                  start=(k == 0), stop=(k == 3)),
                         reads=["cmat", f"{ptres}_{j}{kb}"], writes=[dres])
                    k += 1
            dp, dpr = f32t.next()
            dpv = dp[:, :].rearrange("p (a b) -> p a b", a=4)
            S.op("dve", lambda e: e.tensor_tensor(
                out=dpv, in0=dbk[:, :].rearrange("p (a b) -> p a b", a=4),
                in1=sinke[:, 4 * hh:4 * hh + 4].unsqueeze(2).to_broadcast([128, 4, 128]), op=ALU.add),
                reads=[dres, "sinke"], writes=[dpr])
            S.op("dve", lambda e: e.reciprocal(out=dp[:, :], in_=dp[:, :]), reads=[dpr], writes=[dpr])
            S.op("dve", lambda e: e.tensor_tensor(out=dp[:, :], in0=xb[:, :], in1=dp[:, :], op=ALU.mult),
                 reads=[xres, dpr], writes=[dpr])
            mv = mixed[:, 8 + 4 * hh:12 + 4 * hh, blk * 128:(blk + 1) * 128]
            mres = [f"mix_main{c}" for c in range(8 + 4 * hh, 12 + 4 * hh)]
            S.op("pool", lambda e: e.tensor_tensor(out=mv, in0=dpv, in1=mv, op=ALU.mult),
                 reads=[dpr] + mres, writes=mres)

        def att_roll():
            S.op("pool", lambda e: e.tensor_copy(out=kt[:, 0, :, :], in_=kt[:, 4, :, :]), reads=["kt4"], writes=["kt0"])
            S.op("pool", lambda e: e.tensor_copy(out=va[:, 0, :, :], in_=va[:, 4, :, :]), reads=["va4"], writes=["va0"])

        def attention_s():
            S.op("pool", lambda e: e.dma_start(out=skb[:], in_=skT_d), writes=["skb"], dma="skb")
            S.op("pool", lambda e: e.dma_start(out=vas[:], in_=sv_d), writes=["vas0", "vas1"], dma="vas0")
            qres = [f"q_smp{c}" for c in range(8)]
            sbk, sres = sr.next()
            sv4 = sbk[:, 0:NS * 16].rearrange("p (b j c) -> p b j c", b=NS, j=2)
            for b in range(NS):
                for j in range(2):
                    S.op("pe", lambda e, b=b, j=j: e.matmul(sv4[:, b, j, :], lhsT=skb[64 * j:64 * j + 64, b, :],
                                                           rhs=qs[64 * j:64 * j + 64, b, :], start=True, stop=True),
                         reads=["skb"] + qres, writes=[sres])
            S.op("act", lambda e: e.activation(out=pts[:].rearrange("p b j c -> p (b j c)"), in_=sbk[:, 0:NS * 16],
                                               func=AF.Exp, scale=0.125), reads=[sres], writes=["pts"])
            xb, xres = xr.next()
            dbk, dres = xr.next()
            xv = xb[:, 0:NS * 8].rearrange("p (b c) -> p b c", b=NS)
            dv = dbk[:, 0:NS * 8].rearrange("p (b c) -> p b c", b=NS)
            for b in range(NS):
                for j in range(2):
                    S.op("pe", lambda e, b=b, j=j: e.matmul(xv[:, b, :], lhsT=vas[:, b, j, :], rhs=pts[:, b, j, :],
                                                           start=(j == 0), stop=(j == 1)),
                         reads=["vas0", "vas1", "pts"], writes=[xres])
            for b in range(NS):
                for j in range(2):
                    S.op("pe", lambda e, b=b, j=j: e.matmul(dv[:, b, :], lhsT=cmat[:, ONESA + j, :], rhs=pts[:, b, j, :],
                                                           start=(j == 0), stop=(j == 1)),
                         reads=["cmat", "pts"], writes=[dres])
            S.op("dve", lambda e: e.tensor_tensor(out=sm1[:], in0=qs[:].rearrange("p b c -> p c b"), in1=ksb[:].unsqueeze(1).to_broadcast([128, 8, NS]),
                                                  op=ALU.mult), reads=qres + ["ksb"], writes=["sm1"])
            S.op("act", lambda e: e.activation(func=AF.Copy, out=smb1[:], in_=sm1[:]), reads=["sm1"], writes=["smb1"])
            S.op("dve", lambda e: e.tensor_tensor(out=sm2[:], in0=sm1[:], in1=smb1[:], op=ALU.subtract),
                 reads=["sm1", "smb1"], writes=["sm2"])
            S.op("act", lambda e: e.activation(func=AF.Copy, out=smb2[:], in_=sm2[:]), reads=["sm2"], writes=["smb2"])
            sn, snres = sr.next()
            snv = sn[:, 0:8 * NS]
            S.op("pe", lambda e: e.matmul(snv, lhsT=cmat[:, HALF, :], rhs=smb1[:].rearrange("p c b -> p (c b)"),
                                          start=True, stop=False), reads=["cmat", "smb1"], writes=[snres])
            S.op("pe", lambda e: e.matmul(snv, lhsT=cmat[:, HALF, :], rhs=smb2[:].rearrange("p c b -> p (c b)"),
                                          start=False, stop=True), reads=["cmat", "smb2"], writes=[snres])
            S.op("act", lambda e: e.activation(out=sm1[:].rearrange("p c b -> p (c b)"), in_=snv, func=AF.Exp, scale=0.125),
                 reads=[snres], writes=["sm1"])
            S.op("dve", lambda e: e.scalar_tensor_tensor(out=sm2[:], in0=sm1[:], scalar=2.0, in1=dv.rearrange("p b c -> p c b"),
                                                         op0=ALU.mult, op1=ALU.add),
                 reads=[dres, "sm1"], writes=["sm2"])
            S.op("dve", lambda e: e.tensor_tensor(out=sm2[:], in0=sm2[:], in1=sinke[:].unsqueeze(2).to_broadcast([128, 8, NS]),
                                                  op=ALU.add), reads=["sm2", "sinke"], writes=["sm2"])
            S.op("dve", lambda e: e.reciprocal(out=sm2[:], in_=sm2[:]), reads=["sm2"], writes=["sm2"])
            S.op("dve", lambda e: e.tensor_tensor(out=sm3[:], in0=sm1[:], in1=vsf[:].unsqueeze(1).to_broadcast([128, 8, NS]),
                                                  op=ALU.mult), reads=["sm1", "vsf"], writes=["sm3"])
            S.op("dve", lambda e: e.tensor_tensor(out=sm3[:], in0=xv.rearrange("p b c -> p c b"), in1=sm3[:], op=ALU.add),
                 reads=[xres, "sm3"], writes=["sm3"])
            S.op("dve", lambda e: e.tensor_tensor(out=sm3[:], in0=sm3[:], in1=sm2[:], op=ALU.mult),
                 reads=["sm3", "sm2"], writes=["sm3"])
            mres = [f"mix_smp{c}" for c in range(8, 16)]
            S.op("pool", lambda e: e.tensor_tensor(out=mixs[:, 8:16, :], in0=sm3[:], in1=mixs[:, 8:16, :], op=ALU.mult),
                 reads=["sm3"] + mres, writes=mres)

        def finish_out(g, p):
            n = g.n
            if g.name == "smp":
                for nn in range(16):
                    sq, sqr = bft.next()
                    S.op("act", lambda e, sq=sq, nn=nn: e.activation(out=sq[:, :n], in_=g.xt[:, nn, :n], func=AF.Square),
                         reads=[g.xres(nn)], writes=[sqr])
                    S.op("pe", lambda e, sq=sq, nn=nn: e.matmul(stat[:, :n], lhsT=cmat[:, ONES, :], rhs=sq[:, :n],
                                                                 start=(nn == 0), stop=(nn == 15)),
                         reads=[sqr, "cmat"], writes=["stat"])
            phaseA_rstd(g, from_stat=True)
            for nn in range(16):
                finish_scale(g, p, nn)

        def finish_scale(g, p, nn):
            n = g.n
            S.op("dve", lambda e: e.scalar_tensor_tensor(
                out=g.xt[:, nn, :n], in0=g.xt[:, nn, :n], scalar=gfcol(nn), in1=rstd[:, :n],
                op0=ALU.mult, op1=ALU.mult), reads=[g.xres(nn), "rstd", "cols"], writes=[g.xres(nn)])
            if g.name == "main":
                if nn % 4 == 3:
                    b = p % 2
                    q = nn // 4
                    S.op("sp", lambda e: e.dma_start(out=yT[:, 4 * q:4 * q + 4, p * T:(p + 1) * T],
                                                     in_=xt[b][:, 4 * q:4 * q + 4, :]),
                         reads=[f"xt{b}q{q}"], writes=[f"yo{p}q{q}"], dma=f"yo{b}q{q}")
            elif nn == 15:
                S.op("sp", lambda e: e.dma_start(out=ysT, in_=xts[:]), reads=["xts"], writes=["o_ys"], dma="o_ys")

        ghalo = mk_group("halo", HALO, xt[1], lambda kc: f"xt1q{kc // 4}", hTh, None, None, tabs_h, "tabs_h")
        gsmp = mk_group("smp", NS, xts, lambda kc: "xts", hTs, mixs, qs, tabs_s, "tabs_s")
        gmain = None
        out_res = []

        def state_copies():
            S.op("sp", lambda e: e.dma_start(out=o_sp, in_=sp_nat[:, 1:15, :]), writes=["o_sp"], dma="o_sp")
            S.op("sp", lambda e: e.dma_start(out=o_sk, in_=sk_nat[:, 1:128, :]), writes=["o_sk"], dma="o_sk")
            S.op("sp", lambda e: e.dma_start(out=o_sv, in_=sv_nat[:, 1:128, :]), writes=["o_sv"], dma="o_sv")
        out_res += ["o_sp", "o_sk", "o_sv"]

        def load_tabs(p):
            S.op("sp", lambda e: e.dma_start(out=tabs[:], in_=tabs_d[:, :, HALO + p * T:HALO + (p + 1) * T]),
                 writes=["tabs"], dma="tabs")

        pending_finish = []
        gmains = []
        for p in range(NTILE):
            b = p % 2
            gmains.append(mk_group("main", T, xt[b], lambda kc, b=b: f"xt{b}q{kc // 4}", hT, mixed, qrot, tabs, "tabs"))
        phaseA(gmains[0], alt=True)

        for p in range(NTILE):
            gmain = gmains[p]
            if p == 0:
                load_tabs(0)
            groups = [gmain]
            if p == 0:
                groups = [ghalo, gmain]
            if p + 1 < NTILE and not pending_finish and p > 0:
                load_xt(p + 1)
            if p == NTILE - 1:
                S.op("sp", lambda e: e.dma_start(out=xts[:], in_=xsT), writes=["xts"], dma="xts")
                phaseA(gsmp)
                groups.append(gsmp)

            for ci in range(NCH):
                kind, idx = KINDS[ci]
                if kind == "gp" and idx == 0:
                    flush()
                    if p + 1 < NTILE:
                        load_tabs(p + 1)
                    att_S(p, 0)
                if kind == "o" and idx == 0:
                    flush()
                    att_roll()
                    if p == NTILE - 1:
                        attention_s()
                    if p + 1 < NTILE:
                        while pending_stats:
                            pending_stats.pop(0)()
                        phaseA_rstd(gmains[p + 1])
                issue_w()
                if p == 0 and ci == 4:
                    phaseA(ghalo, alt=True)
                    load_xt(1)
                if p == 0 and ci == 6:
                    sample_window_sums()
                if pending_finish:
                    pg, pp = pending_finish[0]
                    if ci == 2:
                        phaseA_rstd(pg, from_stat=True)
                    if 2 <= ci < 18:
                        finish_scale(pg, pp, ci - 2)
                    if ci == 17:
                        pending_finish.pop()
                        if p + 1 < NTILE:
                            load_xt(p + 1)
                sl = (p * NCH + ci) % WSLOTS
                wres = f"w{sl}"
                for g in groups:
                    n = g.n
                    if g.name == "halo" and kind not in ("u", "k", "v"):
                        continue
                    if kind == "v" and g.name != "smp":
                        nb = n // 128
                        zb, zres = zr.next()
                        for blk in range(nb):
                            for kc in range(16):
                                S.op("pe", lambda e: e.matmul(
                                    zb[:, blk * 128:(blk + 1) * 128], lhsT=g.hT[:, kc, blk * 128:(blk + 1) * 128],
                                    rhs=wsl[sl][:, kc, :], start=(kc == 0), stop=(kc == 15)),
                                    reads=[wres, f"hT_{g.name}"], writes=[zres])
                        s0 = 1 if g.name == "main" else 0
                        zv = zb[:, 0:nb * 128].rearrange("p (a b) -> p a b", a=nb)
                        vres = [f"va{i}" for i in range(s0, s0 + nb)]
                        S.op("act", lambda e: e.activation(func=AF.Copy, out=va[:, s0:s0 + nb, 0, 0:64], in_=zv[:, :, 0:64]),
                             reads=[zres], writes=vres)
                        S.op("dve", lambda e: e.tensor_copy(out=va[:, s0:s0 + nb, 1, 64:128], in_=zv[:, :, 64:128]),
                             reads=[zres], writes=vres)
                        if g.name == "main" and p == NTILE - 1:
                            vf, vfr = f32t.next()
                            S.op("act", lambda e: e.activation(func=AF.Copy, out=vf[:, 0:128], in_=zb[:, T - 128:T]),
                                 reads=[zres], writes=[vfr])
                            S.op("sp", lambda e: e.dma_start(out=v_last, in_=vf[:, 0:128]), reads=[vfr], writes=["o_v"], dma="o_v")
                        continue
                    src = g.hT if ci < NCH_IN else g.mixed
                    if ci < NCH_IN:
                        sres = [f"hT_{g.name}"]
                    else:
                        sres = [f"mix_{g.name}{c}" for c in range(16)]
                    zb, zres = zr.next()
                    for kc in range(16):
                        S.op("pe", lambda e: e.matmul(
                            zb[:, :n], lhsT=wsl[sl][:, kc, :], rhs=src[:, kc, :n], start=(kc == 0), stop=(kc == 15)),
                            reads=[wres] + sres, writes=[zres])
                    evac(g, kind, idx, zb, zres, p)
                if kind == "gp":
                    if idx + 1 < 8:
                        att_S(p, idx + 1)
                    att_X(p, idx)
                if p + 1 < NTILE:
                    if kind == "u":
                        while pending_stats:
                            pending_stats.pop(0)()
                        phaseA_stats(gmains[p + 1], 2 * (ci - U_FIRST), defer=1)
                        phaseA_stats(gmains[p + 1], 2 * (ci - U_FIRST) + 1, defer=1)
                    if kind == "o":
                        phaseA_scale(gmains[p + 1], idx)
                tick()
            flush()
            if p == 0:
                state_copies()
            out_res += [f"yo{p}q{q}" for q in range(4)]
            if p < NTILE - 1:
                pending_finish.append((gmain, p))
            else:
                finish_out(gmain, p)
            if p == NTILE - 1:
                finish_out(gsmp, p)
                out_res += ["o_ys", "o_ksn", "o_vsn", "o_kT", "o_v"]
                S.op("sp", lambda e: e.dma_start(out=upT, in_=uh[:]), reads=[f"uh{i}" for i in range(8)],
                     writes=["o_up"], dma="o_up")
                S.op("sp", lambda e: e.dma_start(out=us_new, in_=us[:]), reads=[f"us{i}" for i in range(8)],
                     writes=["o_usn"], dma="o_usn")
                out_res += ["o_up", "o_usn"]
        S.op("sp", None, reads=out_res)
        S.emit(LIMIT)
    return nc, S.stats


def _weight_layout(w_in, w_out):
    cols = np.zeros((NCH_IN, 128), np.int64)
    pp = np.arange(128)
    hp = np.where(pp < 64, 0, 8)
    dd = pp % 64
    for ci in range(NCH_IN):
        kind, i = KINDS[ci]
        if kind == "gp":
            cols[ci] = 1024 + 128 * i + pp
        elif kind == "u":
            cols[ci] = 128 * i + pp
        elif kind == "ga":
            cols[ci] = 3328 + (i + hp) * 64 + dd
        elif kind == "k":
            cols[ci] = 3072 + pp
        elif kind == "v":
            cols[ci] = 3200 + pp
        elif kind == "q":
            cols[ci] = 2048 + (i + hp) * 64 + dd
    wi = w_in[0][:, cols.reshape(-1)].reshape(16, 128, NCH_IN, 128).transpose(2, 1, 0, 3)
    rows = np.zeros((16, 128), np.int64)
    for kc in range(8):
        rows[kc] = kc * 128 + pp
    for c in range(8):
        rows[8 + c] = 1024 + (c + hp) * 64 + dd
    wo = w_out[0][rows.reshape(-1), :].reshape(16, 128, 16, 128).transpose(2, 1, 0, 3)
    return np.ascontiguousarray(np.concatenate([wi, wo], axis=0), dtype=np.float32)


def _rope_tables(pos):
    half = 8
    inv = np.power(np.float32(500000.0), -np.arange(half, dtype=np.float32) * np.float32(2.0 / 16)).astype(np.float32)
    ang = pos.astype(np.float32)[:, None] * inv[None, :]
    c = np.cos(ang).astype(np.float32).T
    s = np.sin(ang).astype(np.float32).T
    tab = np.zeros((128, 2, pos.shape[0]), np.float32)
    tab[:, 0, :] = 1.0
    for h in range(2):
        base = 64 * h
        tab[base:base + 8, 0] = c
        tab[base + 8:base + 16, 0] = c
        tab[base:base + 8, 1] = -s
        tab[base + 8:base + 16, 1] = s
    return tab


def _const_mats(first_block_has_prev):
    cm = np.zeros((128, 8, 128), np.float32)
    cm[:, 0, :] = 1.0 / 2048.0
    for m in range(128):
        d = m % 64
        if d < 8:
            cm[m + 8, 1, m] = 1.0
        elif d < 16:
            cm[m - 8, 1, m] = 1.0
    cm[:, 2, 0:64] = 2.0
    cm[:, 3, 64:128] = 2.0
    cm[0:64, 4, 0:64] = 1.0
    cm[64:128, 4, 64:128] = 1.0
    s = np.arange(128)[:, None]
    q = np.arange(128)[None, :]
    prev = (q <= s).astype(np.float32)
    cur = (q >= s).astype(np.float32)
    cm[:, 5, :] = prev if first_block_has_prev else 0.0
    cm[:, 6, :] = prev
    cm[:, 7, :] = cur
    return cm


def _pad_v(svn):
    out = np.zeros((128, NS, 2, 128), np.float32)
    t = svn.transpose(1, 0, 2)
    out[:, :, 0, 0:64] = t[:, :, 0:64]
    out[:, :, 1, 64:128] = t[:, :, 64:128]
    return out


_CACHE = {}


def make_in_maps(x_prompt, x_sample, state_pool, state_k_win, state_v_win, norm_g, w_in, w_pool,
                 pool_scale, attn_sinks, w_out, final_norm_g):
    f = lambda a: np.ascontiguousarray(np.asarray(a), dtype=np.float32)
    x_prompt, x_sample = f(x_prompt), f(x_sample)
    state_pool, state_k_win, state_v_win = f(state_pool), f(state_k_win), f(state_v_win)
    norm_g, w_in, w_pool, pool_scale = f(norm_g), f(w_in), f(w_pool), f(pool_scale)
    attn_sinks, w_out, final_norm_g = f(attn_sinks), f(w_out), f(final_norm_g)

    w_all = _weight_layout(w_in, w_out)
    wpool = np.ascontiguousarray(w_pool[0].reshape(4, 2, 128, 256).transpose(2, 0, 1, 3))
    pp = np.arange(128)
    cols = np.zeros((128, 64), np.float32)
    cols[:, 0:16] = norm_g[0].reshape(16, 128).T
    cols[:, 16:32] = final_norm_g.reshape(16, 128).T
    cols[:, 32:40] = pool_scale[0].reshape(8, 128).T
    sk = attn_sinks[0]
    for c in range(8):
        cols[0:64, 40 + c] = sk[c]
        cols[64:128, 40 + c] = sk[8 + c]
    tabs_s = _rope_tables(np.full((NS,), PAST, np.int64))

    in_maps = []
    for c in range(NCORES):
        b, qd = c // 4, c % 4
        t0 = qd * CT
        xx = np.zeros((HALO + CT, D), np.float32)
        if t0 > 0:
            xx[:HALO] = x_prompt[b, t0 - HALO:t0]
        xx[HALO:] = x_prompt[b, t0:t0 + CT]
        xT = np.ascontiguousarray(xx.reshape(HALO + CT, 16, 128).transpose(2, 1, 0))
        xs = x_sample[c * NS:(c + 1) * NS, 0, :]
        xsT = np.ascontiguousarray(xs.reshape(NS, 16, 128).transpose(2, 1, 0))
        pos = np.arange(t0 - HALO, t0 + CT)
        tabs = _rope_tables(np.maximum(pos, 0))
        icnt = np.zeros((128, 8, 16), np.float32)
        for gi, w in enumerate((2, 4, 8, 16)):
            cnt = np.minimum(t0 + np.arange(16) + 1.0, float(w)).astype(np.float32)
            icnt[:, 2 * gi:2 * gi + 2, :] = (np.float32(1.0) / cnt)[None, None, :]
        spn = state_pool[0, c * NS:(c + 1) * NS]
        skn = state_k_win[0, c * NS:(c + 1) * NS].reshape(NS, 128, 128)
        svn = state_v_win[0, c * NS:(c + 1) * NS].reshape(NS, 128, 128)
        in_maps.append({
            "xT": xT, "xsT": xsT, "w_all": w_all, "wpool": wpool,
            "cmat": _const_mats(t0 > 0), "cols": cols, "icnt": icnt,
            "tabs": tabs, "tabs_s": tabs_s,
            "spT": np.ascontiguousarray(spn.reshape(NS, 15, 8, 128).transpose(3, 2, 0, 1)),
            "skT": np.ascontiguousarray(skn.transpose(2, 0, 1)),
            "sv": _pad_v(svn),
            "sp_nat": np.ascontiguousarray(spn), "sk_nat": np.ascontiguousarray(skn),
            "sv_nat": np.ascontiguousarray(svn),
        })

    return in_maps


def kernel(**inputs):
    in_maps = make_in_maps(**inputs)
    if "nc" not in _CACHE:
        _CACHE["nc"] = build_nc()
    nc, _ = _CACHE["nc"]
    res = run_bass_kernel_spmd(nc, in_maps, core_ids=list(range(NCORES)))
    return assemble(res.results)


def assemble(R):
    y_prompt = np.zeros((BATCH, SEQ, D), np.float32)
    y_sample = np.zeros((DB, 1, D), np.float32)
    new_pool_prompt = np.zeros((1, BATCH, 15, 1024), np.float32)
    new_k_prompt = np.zeros((1, BATCH, 128, 2, 64), np.float32)
    new_v_prompt = np.zeros((1, BATCH, 128, 2, 64), np.float32)
    new_pool_sample = np.zeros((1, DB, 15, 1024), np.float32)
    new_k_sample = np.zeros((1, DB, 128, 2, 64), np.float32)
    new_v_sample = np.zeros((1, DB, 128, 2, 64), np.float32)
    for c in range(NCORES):
        b, qd = c // 4, c % 4
        r = R[c]
        y_prompt[b, qd * CT:(qd + 1) * CT] = r["yT"].transpose(2, 1, 0).reshape(CT, D)
        y_sample[c * NS:(c + 1) * NS, 0] = r["ysT"].transpose(2, 1, 0).reshape(NS, D)
        if qd == 3:
            new_pool_prompt[0, b] = r["upT"].transpose(2, 1, 0).reshape(16, 1024)[1:]
            new_k_prompt[0, b] = r["kT_last"].T.reshape(128, 2, 64)
            new_v_prompt[0, b] = r["v_last"].reshape(128, 2, 64)
        sl = slice(c * NS, (c + 1) * NS)
        new_pool_sample[0, sl, :14] = r["o_sp"]
        new_pool_sample[0, sl, 14] = r["us_new"].transpose(2, 1, 0).reshape(NS, 1024)
        new_k_sample[0, sl, :127] = r["o_sk"].reshape(NS, 127, 2, 64)
        new_k_sample[0, sl, 127] = r["ks_new"].T.reshape(NS, 2, 64)
        new_v_sample[0, sl, :127] = r["o_sv"].reshape(NS, 127, 2, 64)
        new_v_sample[0, sl, 127] = r["vs_new"].T.reshape(NS, 2, 64)
    return (y_prompt, y_sample, new_pool_prompt, new_k_prompt, new_v_prompt,
            new_pool_sample, new_k_sample, new_v_sample)
```
